# Optimizing a Trainium2 kernel written in Bass

```python
import math
import jax, jax.numpy as jnp
from jax import lax
import numpy as np

D_MODEL = 2048
BATCH = 4
SEQ = 4096
DEPTH = 1
DEC_BATCH = 16
DEC_SEQ = 2048
PAST_LEN = 128

DA_HEADS = D_MODEL // 512
DA_HEAD_DIM = 128
DA_VDIM = 2 * DA_HEAD_DIM
DA_WIDTH = DA_HEADS * DA_VDIM
NA_HEADS = D_MODEL // 256
NA_HEAD_DIM = 128
NA_WIDTH = NA_HEADS * NA_HEAD_DIM
GRID_W = 64
NA_ROWS = 8
NA_COLS = 16
Q_BLOCK = 128
D_FF = 4 * D_MODEL
ALIBI_MAX_BIAS = 8.0
EPS = 1e-6
IN_W = 3 * DA_WIDTH + 3 * NA_WIDTH + 2 * D_MODEL

kernel_name = "hybrid_diffattn_natten_encoder"


def lambda_init(layer_idx):
    return 0.8 - 0.6 * math.exp(-0.3 * layer_idx)


def rmsnorm(x, g):
    xf = x.astype(jnp.float32)
    y = xf * lax.rsqrt(jnp.mean(xf * xf, axis=-1, keepdims=True) + EPS)
    return (y * g.astype(jnp.float32)).astype(x.dtype)


def diff_attention(q, k, v, lam):
    B, S = q.shape[0], q.shape[1]
    nb = S // Q_BLOCK
    qb = jnp.moveaxis(q.reshape(B, nb, Q_BLOCK, DA_HEADS, 2, DA_HEAD_DIM), 1, 0)
    slopes = jnp.exp2(-ALIBI_MAX_BIAS * jnp.arange(1, DA_HEADS + 1, dtype=jnp.float32) / DA_HEADS)
    key_pos = jnp.arange(S, dtype=jnp.float32)
    scale = DA_HEAD_DIM ** -0.5

    def block(args):
        qi, i = args
        qpos = (i * Q_BLOCK + jnp.arange(Q_BLOCK)).astype(jnp.float32)
        s = jnp.einsum('bqhcd,bkhcd->bhcqk', qi, k).astype(jnp.float32) * scale
        dist = jnp.abs(qpos[:, None] - key_pos[None, :])
        s = s - slopes[None, :, None, None, None] * dist[None, None, None]
        p = jax.nn.softmax(s, axis=-1)
        a = p[:, :, 0] - lam * p[:, :, 1]
        return jnp.einsum('bhqk,bkhe->bqhe', a.astype(v.dtype), v)

    out = lax.map(block, (qb, jnp.arange(nb)))
    return jnp.moveaxis(out, 0, 1).reshape(B, S, DA_HEADS, DA_VDIM)


def neighborhood_attention(q, k, v, rpb):
    B, S = q.shape[0], q.shape[1]
    R = S // GRID_W
    KH = min(NA_ROWS, R)
    qg = q.reshape(B, R, GRID_W, NA_HEADS, NA_HEAD_DIM)
    kg = k.reshape(B, R, GRID_W, NA_HEADS, NA_HEAD_DIM)
    vg = v.reshape(B, R, GRID_W, NA_HEADS, NA_HEAD_DIM)
    rows = jnp.arange(R)
    row_start = jnp.clip(rows - KH // 2, 0, R - KH)
    cols = jnp.arange(GRID_W)
    col_start = jnp.clip(cols - NA_COLS // 2, 0, GRID_W - NA_COLS)
    col_idx = col_start[:, None] + jnp.arange(NA_COLS)[None, :]
    dc = col_idx - cols[:, None] + (NA_COLS - 1)
    scale = NA_HEAD_DIM ** -0.5

    def row(args):
        qr, r, rs = args
        kr = lax.dynamic_slice_in_dim(kg, rs, KH, axis=1)
        vr = lax.dynamic_slice_in_dim(vg, rs, KH, axis=1)
        kw = jnp.take(kr, col_idx, axis=2)
        vw = jnp.take(vr, col_idx, axis=2)
        s = jnp.einsum('bchd,brcjhd->bhcrj', qr, kw).astype(jnp.float32) * scale
        dr = rs + jnp.arange(KH) - r + (NA_ROWS - 1)
        bias = rpb[:, dr[None, :, None], dc[:, None, :]].astype(jnp.float32)
        s = s + bias[None]
        p = jax.nn.softmax(s.reshape(B, NA_HEADS, GRID_W, KH * NA_COLS), axis=-1)
        p = p.reshape(B, NA_HEADS, GRID_W, KH, NA_COLS).astype(v.dtype)
        return jnp.einsum('bhcrj,brcjhd->bchd', p, vw)

    out = lax.map(row, (jnp.moveaxis(qg, 1, 0), rows, row_start))
    return jnp.moveaxis(out, 0, 1).reshape(B, S, NA_WIDTH)


def encoder_layer(l, x, c, w_ada, b_ada, g_mix, w_in, lam_q1, lam_k1, lam_q2, lam_k2,
                  g_subln, rpb, w_pa, w_pb, w_out, g_mlp, w1, w2):
    B, S, _ = x.shape
    ada = (c @ w_ada + b_ada)[:, None, :]
    sh1, sc1, gt1, sh2, sc2, gt2 = jnp.split(ada, 6, axis=-1)

    h = rmsnorm(x, g_mix) * (1 + sc1) + sh1
    proj = h @ w_in
    splits = np.cumsum([DA_WIDTH, DA_WIDTH, DA_WIDTH, NA_WIDTH, NA_WIDTH, NA_WIDTH, D_MODEL])
    qa, ka, va, qn, kn, vn, ga, gb = jnp.split(proj, [int(s_) for s_ in splits], axis=-1)

    lam_f = lambda_init(l)
    lam = (jnp.exp(jnp.sum(lam_q1.astype(jnp.float32) * lam_k1.astype(jnp.float32)))
           - jnp.exp(jnp.sum(lam_q2.astype(jnp.float32) * lam_k2.astype(jnp.float32))) + lam_f)
    oa = diff_attention(qa.reshape(B, S, DA_HEADS, 2, DA_HEAD_DIM),
                        ka.reshape(B, S, DA_HEADS, 2, DA_HEAD_DIM),
                        va.reshape(B, S, DA_HEADS, DA_VDIM), lam)
    oa = (rmsnorm(oa, g_subln) * (1.0 - lam_f)).reshape(B, S, DA_WIDTH)

    on = neighborhood_attention(qn.reshape(B, S, NA_HEADS, NA_HEAD_DIM),
                                kn.reshape(B, S, NA_HEADS, NA_HEAD_DIM),
                                vn.reshape(B, S, NA_HEADS, NA_HEAD_DIM), rpb)

    merged = jax.nn.sigmoid(ga) * (oa @ w_pa) + jax.nn.sigmoid(gb) * (on @ w_pb)
    x = x + gt1 * (merged @ w_out)

    h2 = rmsnorm(x, g_mlp) * (1 + sc2) + sh2
    x = x + gt2 * (jnp.square(jax.nn.relu(h2 @ w1)) @ w2)
    return x


def run_trunk(x, c, w_ada, b_ada, g_mix, w_in, lam_q1, lam_k1, lam_q2, lam_k2,
              g_subln, rpb, w_pa, w_pb, w_out, g_mlp, w1, w2, g_final):
    for l in range(DEPTH):
        x = encoder_layer(l, x, c, w_ada[l], b_ada[l], g_mix[l], w_in[l], lam_q1[l], lam_k1[l],
                          lam_q2[l], lam_k2[l], g_subln[l], rpb[l], w_pa[l], w_pb[l], w_out[l],
                          g_mlp[l], w1[l], w2[l])
    return rmsnorm(x, g_final)


def setup_inputs(seed: int = 0) -> dict:
    key = jax.random.key(seed)
    ks = jax.random.split(key, 24)
    f32 = jnp.float32

    def nrm(k, shape, s):
        return jax.random.normal(k, shape, f32) * s

    return {
        "x_prompt": nrm(ks[0], (BATCH, SEQ, D_MODEL), 1.0),
        "x_sample": nrm(ks[1], (DEC_BATCH, DEC_SEQ, D_MODEL), 1.0),
        "c_prompt": nrm(ks[2], (BATCH, D_MODEL), 1.0),
        "c_sample": nrm(ks[3], (DEC_BATCH, D_MODEL), 1.0),
        "w_ada": nrm(ks[4], (DEPTH, D_MODEL, 6 * D_MODEL), 0.5 * D_MODEL ** -0.5),
        "b_ada": nrm(ks[5], (DEPTH, 6 * D_MODEL), 0.02),
        "g_mix": 1.0 + nrm(ks[6], (DEPTH, D_MODEL), 0.02),
        "w_in": nrm(ks[7], (DEPTH, D_MODEL, IN_W), D_MODEL ** -0.5),
        "lam_q1": nrm(ks[8], (DEPTH, DA_HEAD_DIM), 0.1),
        "lam_k1": nrm(ks[9], (DEPTH, DA_HEAD_DIM), 0.1),
        "lam_q2": nrm(ks[10], (DEPTH, DA_HEAD_DIM), 0.1),
        "lam_k2": nrm(ks[11], (DEPTH, DA_HEAD_DIM), 0.1),
        "g_subln": 1.0 + nrm(ks[12], (DEPTH, DA_VDIM), 0.02),
        "rpb": nrm(ks[13], (DEPTH, NA_HEADS, 2 * NA_ROWS - 1, 2 * NA_COLS - 1), 0.1),
        "w_pa": nrm(ks[14], (DEPTH, DA_WIDTH, D_MODEL), DA_WIDTH ** -0.5),
        "w_pb": nrm(ks[15], (DEPTH, NA_WIDTH, D_MODEL), NA_WIDTH ** -0.5),
        "w_out": nrm(ks[16], (DEPTH, D_MODEL, D_MODEL), D_MODEL ** -0.5),
        "g_mlp": 1.0 + nrm(ks[17], (DEPTH, D_MODEL), 0.02),
        "w1": nrm(ks[18], (DEPTH, D_MODEL, D_FF), D_MODEL ** -0.5),
        "w2": nrm(ks[19], (DEPTH, D_FF, D_MODEL), D_FF ** -0.5),
        "g_final": 1.0 + nrm(ks[20], (D_MODEL,), 0.02),
    }


def reference(x_prompt, x_sample, c_prompt, c_sample, w_ada, b_ada, g_mix, w_in,
              lam_q1, lam_k1, lam_q2, lam_k2, g_subln, rpb, w_pa, w_pb, w_out,
              g_mlp, w1, w2, g_final):
    y_prompt = run_trunk(x_prompt, c_prompt, w_ada, b_ada, g_mix, w_in, lam_q1, lam_k1,
                         lam_q2, lam_k2, g_subln, rpb, w_pa, w_pb, w_out, g_mlp, w1, w2, g_final)
    y_sample = run_trunk(x_sample, c_sample, w_ada, b_ada, g_mix, w_in, lam_q1, lam_k1,
                         lam_q2, lam_k2, g_subln, rpb, w_pa, w_pb, w_out, g_mlp, w1, w2, g_final)
    return (y_prompt, y_sample)
```

```python
import math
import os
from contextlib import ExitStack

import numpy as np
import concourse.bass as bass
import concourse.mybir as mybir
from concourse.bass_utils import run_bass_kernel_spmd

F32 = mybir.dt.float32
BF16 = mybir.dt.bfloat16
AF = mybir.ActivationFunctionType
ALU = mybir.AluOpType

D = 2048
NJOB = 3
T = 2048
EPS = 1e-6
LAM_INIT = 0.8 - 0.6 * math.exp(0.0)
SLOPES = [2.0 ** (-8.0 * (h + 1) / 4) for h in range(4)]
QSCALE = 128 ** -0.5
NEG = -30000.0
ENGS = ("sp", "act", "dve", "pool", "pe")


def _na_tile_idx(R, qrows, krows):
    out = -np.ones((128, 128), np.int64)
    for qi in range(128):
        r = qrows[qi // 64]
        c = qi % 64
        if r < 0 or r >= R:
            continue
        rs = min(max(r - 4, 0), R - 8)
        cs = min(max(c - 8, 0), 64 - 16)
        for ki in range(128):
            kr = krows[ki // 64]
            kc = ki % 64
            if kr < 0 or kr >= R:
                continue
            if rs <= kr < rs + 8 and cs <= kc < cs + 16:
                out[ki, qi] = (kr - r + 7) * 31 + (kc - c + 15)
    return out


NA_S_REL = {0: [0, 1, 2, 3], 1: [-1, 0, 1, 2], 14: [-2, -1, 0, 1], 15: [-3, -2, -1, 0]}
NA_P_REL = {0: [-2, -1, 0, 1, 2, 3], 15: [-3, -2, -1, 0, 1, 2]}


def na_rel(jobtype, j):
    d = NA_S_REL if jobtype == "s" else NA_P_REL
    return d.get(j, [-2, -1, 0, 1, 2])


def na_groups(jobtype):
    if jobtype == "s":
        keys = {0: "j0", 1: "j1", 14: "j14", 15: "j15"}
    else:
        keys = {0: "j0", 1: "j1", 14: "j14", 15: "j15"}
    order = ["j0", "j1", "int", "j14", "j15"]
    rep = {"j0": 0, "j1": 1, "int": 7, "j14": 14, "j15": 15}
    offs = {}
    o = 0
    for k in order:
        offs[k] = o
        o += len(na_rel(jobtype, rep[k]))
    return keys, order, rep, offs, o


def build_na_idx(jobtype, half):
    keys, order, rep, offs, ntile = na_groups(jobtype)
    tiles = []
    for k in order:
        j = rep[k]
        if jobtype == "s":
            R = 32
            jg = j
        else:
            R = 64
            jg = half * 16 + j
        for rel in na_rel(jobtype, j):
            kt = jg + rel
            tiles.append(_na_tile_idx(R, [2 * jg, 2 * jg + 1], [2 * kt, 2 * kt + 1]))
    return np.stack(tiles, 0)


def build_tables(half):
    kp = np.arange(128, dtype=np.float32)[:, None]
    qf = np.arange(512, dtype=np.float32)[None, :]
    dconst = np.zeros((128, 5, 512), np.float32)
    dconst[:, 0, :] = qf - kp
    for i in range(4):
        dconst[:, 1 + i, :] = np.abs(qf - kp - 128.0 * i)
    tabc = np.zeros((128, 4 * 33), np.float32)
    for h in range(4):
        for m in range(33):
            tabc[:, h * 33 + m] = -SLOPES[h] * m * 128.0
    tabo = np.zeros((128, 4 + 256), np.float32)
    for h in range(4):
        tabo[:, h] = SLOPES[h] if half == 0 else -SLOPES[h]
        for g in range(4):
            for kt in range(16):
                m = (16 + kt - 4 * g) if half == 0 else (16 + 4 * g - kt)
                tabo[:, 4 + h * 64 + g * 16 + kt] = -SLOPES[h] * m * 128.0
    slp = np.zeros((128, 8), np.float32)
    for h in range(4):
        slp[:, h] = SLOPES[h]
        slp[:, 4 + h] = -SLOPES[h]
    return dconst, tabc, tabo, slp


class Sem:
    def __init__(self, h):
        self.h = h
        self.n = 0


class Slot:
    __slots__ = ("ap", "wr", "rd")

    def __init__(self, ap=None):
        self.ap = ap
        self.wr = {}
        self.rd = {}


class Builder:
    def __init__(self, nc, es):
        self.nc = nc
        self.es = es
        self.q = {k: [] for k in ENGS}
        self.seen = {k: {} for k in ENGS}
        self.prog = {k: self.sem("prog_" + k) for k in ("act", "dve", "pool", "pe")}
        self.dsems = []
        self.nins = 0

    def sem(self, name):
        return Sem(self.es.enter_context(self.nc.semaphore(name)))

    def dsem(self, name):
        s = self.sem(name)
        self.dsems.append(s)
        return s

    def wait(self, eng, tok):
        if tok is None:
            return
        s, v = tok[0], tok[1]
        if self.seen[eng].get(id(s), 0) >= v:
            return
        self.seen[eng][id(s)] = v
        self.q[eng].append(lambda e, s=s, v=v: e.wait_ge(s.h, v))
        self.nins += 1

    def _deps(self, eng, reads, writes, extra, is_dma, nowaw=False):
        for s in reads:
            for t in list(s.wr.values()):
                self.wait(eng, t)
        for s in writes:
            for t in list(s.rd.values()):
                self.wait(eng, t)
            if not nowaw:
                for t in list(s.wr.values()):
                    self.wait(eng, t)
        for t in extra:
            self.wait(eng, t)

    def _record(self, tok, reads, writes, cont=()):
        k = id(tok[0])
        for s in reads:
            s.rd[k] = tok
        for s in writes:
            s.wr[k] = tok
        for s in cont:
            s.wr[k] = tok

    def op(self, eng, fn, reads=(), writes=(), extra=(), nowaw=False):
        self._deps(eng, reads, writes, extra, False, nowaw)
        self.nins += 1
        ps = self.prog[eng]
        ps.n += 1
        tok = (ps, ps.n, eng)
        self.q[eng].append(lambda e, fn=fn, ps=ps: fn(e).then_inc(ps.h, 1))
        self._record(tok, reads, writes)
        return tok

    def group(self, eng, fns, reads=(), writes=(), cont=(), extra=()):
        self._deps(eng, reads, writes, extra, False)
        for fn in fns[:-1]:
            self.q[eng].append(fn)
        self.nins += len(fns)
        ps = self.prog[eng]
        ps.n += 1
        tok = (ps, ps.n, eng)
        fn = fns[-1]
        self.q[eng].append(lambda e, fn=fn, ps=ps: fn(e).then_inc(ps.h, 1))
        self._record(tok, reads, writes, cont)
        return tok

    def dma(self, eng, out, in_, dsem, reads=(), writes=(), extra=()):
        self._deps(eng, reads, writes, extra, True)
        dsem.n += 16
        tok = (dsem, dsem.n, "dma")
        self.q[eng].append(lambda e, o=out, i=in_, d=dsem: e.dma_start(out=o, in_=i).then_inc(d.h, 16))
        self.nins += 1
        self._record(tok, reads, writes)
        return tok

    def batch_fix(self, dsem, slots):
        tok = (dsem, dsem.n, "dma")
        for s in slots:
            if id(dsem) in s.wr:
                s.wr[id(dsem)] = tok
            if id(dsem) in s.rd:
                s.rd[id(dsem)] = tok

    def barrier(self):
        toks = [(self.prog[k], self.prog[k].n, k) for k in self.prog]
        toks += [(d, d.n, "dma") for d in self.dsems]
        for eng in ENGS:
            for t in toks:
                if t[1] > 0:
                    self.wait(eng, t)

    def emit(self, block):
        q = self.q

        @block.sync
        def _(e):
            for fn in q["sp"]:
                fn(e)

        @block.scalar
        def _(e):
            for fn in q["act"]:
                fn(e)

        @block.vector
        def _(e):
            for fn in q["dve"]:
                fn(e)

        @block.gpsimd
        def _(e):
            for fn in q["pool"]:
                fn(e)

        @block.tensor
        def _(e):
            for fn in q["pe"]:
                fn(e)


class Arena:
    def __init__(self, ap, n):
        self.ap = ap
        self.n = n
        self.off = 0

    def alloc(self, shape, dt):
        cnt = int(np.prod(shape))
        nb = cnt * (2 if dt == F32 else 1)
        nb = (nb + 15) // 16 * 16
        assert self.off + nb <= self.n, ("SBUF arena overflow", self.off, nb, self.n)
        v = self.ap[:, self.off:self.off + (cnt * (2 if dt == F32 else 1))]
        self.off += nb
        if dt == F32:
            v = v.bitcast(F32)
        if len(shape) == 2:
            v = v.rearrange("p (a b) -> p a b", a=shape[0], b=shape[1])
        elif len(shape) == 3:
            v = v.rearrange("p (a b c) -> p a b c", a=shape[0], b=shape[1], c=shape[2])
        return v

    def slot(self, shape, dt):
        return Slot(self.alloc(shape, dt))

    def slot_top(self, shape, dt):
        cnt = int(np.prod(shape)) * (2 if dt == F32 else 1)
        nb = (cnt + 15) // 16 * 16
        self.n -= nb
        assert self.n >= self.off
        save_off, save_n = self.off, self.n
        self.off, self.n = self.n, self.n + nb
        v = self.alloc(shape, dt)
        self.off, self.n = save_off, save_n
        return Slot(v)


ARENA_ELEMS = 101 * 1024


def build_nc(stage=99, dbg=False):
    nc = bass.Bass("TRN2", target_bir_lowering=False)
    es = ExitStack()

    def din(name, shape, dt=F32):
        return nc.dram_tensor(name, list(shape), dt, kind="ExternalInput").ap()

    def dscr(name, shape, dt=BF16, out=False):
        kind = "ExternalOutput" if (out and dbg) else "Internal"
        return nc.dram_tensor(name, list(shape), dt, kind=kind).ap()

    xq = din("xq", [NJOB, T, D])
    xo = din("xo", [T, D])
    cT_d = din("cT", [128, 64])
    w_ada = din("w_ada", [D, 6 * D])
    b_adaT_d = din("b_adaT", [128, 96])
    g_mixT_d = din("g_mixT", [128, 16])
    g_mlpT_d = din("g_mlpT", [128, 16])
    g_final_d = din("g_final", [1, D])
    g_subln_d = din("g_subln", [1, 256])
    lam4_d = din("lam4", [4, 128])
    w_in = din("w_in", [D, 10240])
    w_pa = din("w_pa", [1024, D])
    w_pb = din("w_pb", [1024, D])
    w_out = din("w_out", [D, D])
    w1 = din("w1", [D, 4 * D])
    w2 = din("w2", [4 * D, D])
    nab_s_d = din("nab_s", [8, 128, 21 * 128])
    nab_p_d = din("nab_p", [8, 128, 27 * 128])
    dconst_d = din("dconst", [128, 5 * 512])
    tabc_d = din("tabc", [128, 132])
    tabo_d = din("tabo", [128, 260])
    slp_d = din("slp", [128, 8])
    y = nc.dram_tensor("y", [NJOB, T, D], F32, kind="ExternalOutput").ap()

    wb_in = dscr("wb_in", [20, 128, 8192])
    wb_pa = dscr("wb_pa", [4, 128, 4096])
    wb_pb = dscr("wb_pb", [4, 128, 4096])
    wb_out = dscr("wb_out", [4, 128, 8192])
    wb_1 = dscr("wb_1", [16, 128, 8192])
    wb_2 = dscr("wb_2", [16, 128, 8192])
    SK = [4096, 2048, 2048]
    SKN = [2560, 2048, 2048]
    qaT = [dscr(f"qaT{j}", [1024, T], out=True) for j in range(NJOB)]
    kaT = [dscr(f"kaT{j}", [1024, SK[j]], out=True) for j in range(NJOB)]
    va = [dscr(f"va{j}", [SK[j], 1024], out=True) for j in range(NJOB)]
    qnT = [dscr(f"qnT{j}", [1024, T], out=True) for j in range(NJOB)]
    knT = [dscr(f"knT{j}", [1024, SKN[j]], out=True) for j in range(NJOB)]
    vn = [dscr(f"vn{j}", [SKN[j], 1024], out=True) for j in range(NJOB)]
    gT = [dscr(f"gT{j}", [4096, T], out=True) for j in range(NJOB)]
    oa_s = [dscr(f"oa{j}", [T, 1024], out=True) for j in range(NJOB)]
    on_s = [dscr(f"on{j}", [T, 1024], out=True) for j in range(NJOB)]
    dbg_ada = nc.dram_tensor("dbg_ada", [128, 96 * 4], F32, kind="ExternalOutput").ap() if dbg else None

    arena_t = es.enter_context(nc.sbuf_tensor("arena", [128, ARENA_ELEMS], BF16))
    AR = Arena(arena_t[:], ARENA_ELEMS)
    banks = [Slot(es.enter_context(nc.psum_tensor(f"bank{i}", [128, 512], F32))[:]) for i in range(8)]
    B = Builder(nc, es)

    DS = [B.dsem(f"ds{i}") for i in range(24)]
    wsem_in = [B.sem(f"wsin{i}") for i in range(4)]
    wsem_pa = B.sem("wspa")
    wsem_pb = B.sem("wspb")
    wsem_out = B.sem("wsout")
    wsem_1 = B.sem("ws1")
    wsem_2 = B.sem("ws2")

    W_in = [Slot() for _ in range(4)]
    W_pa, W_pb, W_out, W_1, W_2 = Slot(), Slot(), Slot(), Slot(), Slot()
    S_qaT = [Slot() for _ in range(NJOB)]
    S_kaT = [Slot() for _ in range(NJOB)]
    S_va = [Slot() for _ in range(NJOB)]
    S_qnT = [Slot() for _ in range(NJOB)]
    S_knT = [Slot() for _ in range(NJOB)]
    S_vn = [Slot() for _ in range(NJOB)]
    S_gT = [Slot() for _ in range(NJOB)]
    S_oa = [Slot() for _ in range(NJOB)]
    S_on = [Slot() for _ in range(NJOB)]
    S_y = Slot()

    ident = AR.slot([128], BF16)
    identf = AR.slot([128], F32)
    onesf = AR.slot([128], F32)
    epsc = AR.slot([1], F32)
    adaT = AR.slot([96, 4], F32)
    s1p = AR.slot([16, 4], F32)
    s2p = AR.slot([16, 4], F32)
    g_mixT = AR.slot([16], F32)
    g_mlpT = AR.slot([16], F32)
    b_adaT = AR.slot([96], F32)
    gf_bc = AR.slot([D], F32)
    gsub_bc = AR.slot([256], F32)
    neglam = AR.slot([1], F32)
    gt1_bc = AR.slot([D], F32)
    gt2_bc = AR.slot([D], F32)
    lamt = AR.slot([8], F32)

    def mk_ident(slot):
        B.op("pool", lambda e, a=slot.ap: e.memset(a, 0.0), writes=[slot])
        B.op("pool", lambda e, a=slot.ap: e.affine_select(out=a, in_=a, compare_op=ALU.not_equal, fill=1.0,
                                                          base=0, pattern=[[-1, 128]], channel_multiplier=1),
             reads=[slot], writes=[slot])

    mk_ident(ident)
    mk_ident(identf)
    B.op("pool", lambda e: e.memset(onesf.ap, 1.0), writes=[onesf])
    B.op("pool", lambda e: e.memset(epsc.ap, EPS), writes=[epsc])
    B.op("pool", lambda e: e.memset(lamt.ap, 0.0), writes=[lamt])

    def conv(dst, src, k, sem, slot):
        B.dma("pool", dst.rearrange("p (k n) -> p k n", k=k, n=512), src.rearrange("(k p) n -> p k n", p=128), sem, writes=[slot])

    for nb in (2, 3, 4, 0, 1, 5, 6, 7, 8, 9, 10, 11, 12, 13, 14, 15, 16, 17, 18, 19):
        conv(wb_in[nb], w_in[:, nb * 512:(nb + 1) * 512], 16, wsem_in[nb // 5], W_in[nb // 5])
    def conv_rest(gate):
        B.wait("pool", gate)
        for nb in range(4):
            conv(wb_pa[nb], w_pa[:, nb * 512:(nb + 1) * 512], 8, wsem_pa, W_pa)
            conv(wb_pb[nb], w_pb[:, nb * 512:(nb + 1) * 512], 8, wsem_pb, W_pb)
        for nb in range(4):
            conv(wb_out[nb], w_out[:, nb * 512:(nb + 1) * 512], 16, wsem_out, W_out)
        for nb in range(16):
            conv(wb_1[nb], w1[:, nb * 512:(nb + 1) * 512], 16, wsem_1, W_1)
        for qq in range(4):
            for cg in range(4):
                conv(wb_2[qq * 4 + cg], w2[qq * 2048:(qq + 1) * 2048, cg * 512:(cg + 1) * 512], 16, wsem_2, W_2)

    B.dma("sp", g_mixT.ap, g_mixT_d, DS[0], writes=[g_mixT])
    B.dma("sp", g_mlpT.ap, g_mlpT_d, DS[0], writes=[g_mlpT])
    B.dma("sp", b_adaT.ap, b_adaT_d, DS[0], writes=[b_adaT])
    B.dma("sp", gf_bc.ap, g_final_d[0, :].partition_broadcast(128), DS[0], writes=[gf_bc])
    B.dma("sp", gsub_bc.ap, g_subln_d[0, :].partition_broadcast(128), DS[0], writes=[gsub_bc])
    B.batch_fix(DS[0], [g_mixT, g_mlpT, b_adaT, gf_bc, gsub_bc])

    m0 = AR.off
    lamv = AR.slot([4, 128], F32)
    ljunk = AR.slot([128], F32)
    for i in range(4):
        B.dma("sp", lamv.ap[:, i, :], lam4_d[i, :].partition_broadcast(128), DS[1], writes=[lamv])
    B.op("dve", lambda e: e.scalar_tensor_tensor(out=ljunk.ap, in0=lamv.ap[:, 0, :], scalar=1.0, in1=lamv.ap[:, 1, :],
                                                 op0=ALU.mult, op1=ALU.mult, accum_out=lamt.ap[:, 0:1]),
         reads=[lamv, lamt], writes=[ljunk, lamt])
    B.op("dve", lambda e: e.scalar_tensor_tensor(out=ljunk.ap, in0=lamv.ap[:, 2, :], scalar=1.0, in1=lamv.ap[:, 3, :],
                                                 op0=ALU.mult, op1=ALU.mult, accum_out=lamt.ap[:, 1:2]),
         reads=[lamv, lamt], writes=[ljunk, lamt])
    B.op("act", lambda e: e.activation(out=lamt.ap[:, 2:4], in_=lamt.ap[:, 0:2], func=AF.Exp), reads=[lamt], writes=[lamt])
    B.op("dve", lambda e: e.tensor_tensor(out=lamt.ap[:, 4:5], in0=lamt.ap[:, 3:4], in1=lamt.ap[:, 2:3], op=ALU.subtract),
         reads=[lamt], writes=[lamt])
    B.op("dve", lambda e: e.tensor_scalar(out=neglam.ap, in0=lamt.ap[:, 4:5], scalar1=-LAM_INIT, scalar2=None, op0=ALU.add),
         reads=[lamt], writes=[neglam])
    B.op("dve", lambda e: e.tensor_scalar(out=gsub_bc.ap, in0=gsub_bc.ap, scalar1=1.0 - LAM_INIT, scalar2=None, op0=ALU.mult),
         reads=[gsub_bc], writes=[gsub_bc])

    cT = AR.slot_top([16, 4], F32)
    wad = [AR.slot_top([16, 256], F32) for _ in range(2)]
    tmp16 = AR.slot_top([16, 4], F32)
    base_off2 = AR.off
    B.dma("sp", cT.ap.rearrange("p a b -> p (a b)"), cT_d, DS[16], writes=[cT])
    ada_state = {"i": 0, "issued": 0}
    ADA_ORDER = list(range(48))

    def ada_issue():
        i = ada_state["issued"]
        if i >= 48:
            return
        ht = ADA_ORDER[i]
        wt = wad[i % 2]
        B.dma("sp", wt.ap, w_ada[:, ht * 256:(ht + 1) * 256].rearrange("(k p) n -> p k n", p=128), DS[2 + i % 2], writes=[wt])
        ada_state["issued"] += 1

    def ada_half_tiles(n, bank_fn):
        for _ in range(n):
            i = ada_state["i"]
            if i >= 48:
                return
            if ada_state["issued"] <= i:
                ada_issue()
            ht = ADA_ORDER[i]
            wt = wad[i % 2]
            bk = bank_fn()
            fns = []
            for m_ in range(2):
                for kc in range(16):
                    fns.append(lambda e, bk=bk, wt=wt, m_=m_, kc=kc: e.matmul(
                        bk.ap[:, m_ * 4:m_ * 4 + 4], wt.ap[:, kc, m_ * 128:(m_ + 1) * 128], cT.ap[:, kc, :],
                        start=(kc == 0), stop=(kc == 15)))
            B.group("pe", fns, reads=[wt, cT], writes=[bk])
            ada_state["i"] += 1
            ada_issue()
            B.op("dve", lambda e, bk=bk, ht=ht: e.tensor_copy(
                out=adaT.ap[:, ht * 2:(ht + 1) * 2, :], in_=bk.ap[:, 0:8].rearrange("p (a b) -> p a b", a=2, b=4)),
                reads=[bk], writes=[adaT], nowaw=True)

    def ada_finish(lo, hi, parts):
        for s_ in range(4):
            B.op("dve", lambda e, s_=s_: e.tensor_tensor(out=adaT.ap[:, lo:hi, s_], in0=adaT.ap[:, lo:hi, s_],
                                                         in1=b_adaT.ap[:, lo:hi], op=ALU.add),
                 reads=[adaT, b_adaT], writes=[adaT])
        for (dst, lo_, g) in parts:
            B.op("dve", lambda e, lo_=lo_: e.tensor_scalar(out=tmp16.ap, in0=adaT.ap[:, lo_:lo_ + 16, :], scalar1=1.0, scalar2=None,
                                                           op0=ALU.add), reads=[adaT], writes=[tmp16])
            for s_ in range(4):
                B.op("dve", lambda e, dst=dst, g=g, s_=s_: e.tensor_tensor(out=dst.ap[:, :, s_], in0=tmp16.ap[:, :, s_], in1=g.ap,
                                                                           op=ALU.mult), reads=[tmp16, g], writes=[dst])

    _pb = [0]

    def _p0bank():
        _pb[0] += 1
        return banks[_pb[0] % 2]

    ada_issue()
    ada_half_tiles(16, _p0bank)
    ada_finish(0, 32, [(s1p, 16, g_mixT)])
    B.barrier()
    AR.off = base_off2

    def make_gt_bc(js):
        m = AR.off
        diag = [AR.slot([128], F32) for _ in range(2)]
        i = 0
        for (dst, lo) in ((gt1_bc, 32), (gt2_bc, 80)):
            for kc4 in range(4):
                bk = banks[i % 2]
                for kk in range(4):
                    kc = kc4 * 4 + kk
                    dg = diag[kc % 2]
                    B.op("dve", lambda e, dg=dg, lo=lo, kc=kc: e.tensor_scalar(
                        out=dg.ap, in0=identf.ap, scalar1=adaT.ap[:, lo + kc, js:js + 1], scalar2=None, op0=ALU.mult),
                        reads=[identf, adaT], writes=[dg])
                    B.group("pe", [lambda e, bk=bk, dg=dg, kk=kk: e.matmul(
                        bk.ap[:, kk * 128:(kk + 1) * 128], onesf.ap, dg.ap, start=True, stop=True)],
                        reads=[onesf, dg], writes=[bk] if kk == 0 else [], cont=[bk] if kk > 0 else [])
                B.op("act", lambda e, bk=bk, dst=dst, kc4=kc4: e.activation(
                    out=dst.ap[:, kc4 * 512:(kc4 + 1) * 512], in_=bk.ap, func=AF.Copy), reads=[bk], writes=[dst])
                i += 1
        B.barrier()
        AR.off = m

    def phase_A(xsrc, js, hT):
        m = AR.off
        xt = [AR.slot([D], F32) for _ in range(2)]
        xn = [AR.slot([D], BF16) for _ in range(2)]
        junk = AR.slot([D], BF16)
        st = AR.slot([16, 4], F32)
        KA = int(os.environ.get("KA", "9"))
        B.op("pool", lambda e: e.memset(st.ap, 0.0), writes=[st])
        for tt in range(16):
            if KA <= 0:
                continue
            x_ = xt[tt % 2]
            B.dma("sp", x_.ap, xsrc[tt * 128:(tt + 1) * 128, :], DS[tt % 2], writes=[x_])
            B.op("act", lambda e, x_=x_, tt=tt: e.activation(out=junk.ap, in_=x_.ap, func=AF.Square,
                                                             accum_out=st.ap[:, tt, 0:1]),
                 reads=[x_, st], writes=[junk, st])
            B.op("act", lambda e, tt=tt: e.activation(out=st.ap[:, tt, 1:2], in_=st.ap[:, tt, 0:1], func=AF.Ln,
                                                      bias=epsc.ap, scale=1.0 / D), reads=[st, epsc], writes=[st])
            B.op("act", lambda e, tt=tt: e.activation(out=st.ap[:, tt, 2:3], in_=st.ap[:, tt, 1:2], func=AF.Exp, scale=-0.5),
                 reads=[st], writes=[st])
            if KA <= 1:
                continue
            xn_ = xn[tt % 2]
            B.op("act", lambda e, x_=x_, xn_=xn_, tt=tt: e.activation(
                out=xn_.ap, in_=x_.ap, func=AF.Copy, scale=st.ap[:, tt, 2:3]),
                reads=[x_, st], writes=[xn_])
            if KA <= 2:
                continue
            for k4 in range(4):
                bk = banks[(tt * 4 + k4) % 8]
                pv = bk.ap.bitcast(BF16)
                fns = [lambda e, pv=pv, xn_=xn_, kk=kk, k4=k4: e.transpose(
                    pv[:, kk * 128:(kk + 1) * 128], xn_.ap[:, (k4 * 4 + kk) * 128:(k4 * 4 + kk + 1) * 128], ident.ap)
                    for kk in range(4)]
                B.group("pe", fns, reads=[xn_, ident], writes=[bk])
                if KA <= 3:
                    continue
                for kk in range(4):
                    kc = k4 * 4 + kk
                    dst = hT.ap[:, kc, tt * 128:(tt + 1) * 128]
                    if KA == 6:
                        B.op("act", lambda e, pv=pv, kk=kk, kc=kc, dst=dst: e.activation(
                            out=dst, in_=pv[:, kk * 128:(kk + 1) * 128], func=AF.Identity,
                            bias=adaT.ap[:, kc, js:js + 1], scale=s1p.ap[:, kc, js:js + 1]),
                            reads=[bk, adaT, s1p], writes=[hT], nowaw=True)
                    else:
                        B.op("dve", lambda e, pv=pv, kk=kk, kc=kc, dst=dst: e.tensor_scalar(
                            out=dst, in0=pv[:, kk * 128:(kk + 1) * 128], scalar1=s1p.ap[:, kc, js:js + 1],
                            scalar2=adaT.ap[:, kc, js:js + 1], op0=ALU.mult, op1=ALU.add),
                            reads=[bk, adaT, s1p], writes=[hT], nowaw=True)
        B.barrier()
        AR.off = m

    ps = {"i": 0, "bi": 0, "si": 0}

    class WStream:
        def __init__(self, bufs, sems, specs):
            self.bufs, self.sems, self.specs = bufs, sems, specs
            self.i = 0
            self.u = 0

        def get(self):
            nbuf = len(self.bufs)
            while self.i < len(self.specs) and self.i < self.u + nbuf - 1:
                src, rd, nk = self.specs[self.i]
                wt = self.bufs[self.i % nbuf]
                dst = wt.ap.rearrange("p a b -> p (a b)") if nk == 16 else wt.ap[:, 0:nk, :].rearrange("p a b -> p (a b)")
                B.dma("sp", dst, src, self.sems[self.i % nbuf], reads=[rd], writes=[wt])
                self.i += 1
            wt = self.bufs[self.u % nbuf]
            self.u += 1
            return wt

    wstream = [None]

    def load_w(wtiles, nb):
        return wstream[0].get()

    def evac(kind, o, bk, sg, par):
        if kind == "sig":
            B.op("act", lambda e: e.activation(out=o, in_=bk.ap, func=AF.Sigmoid), reads=[bk], writes=[sg], nowaw=True)
        elif kind == "q":
            if par == 0:
                B.op("act", lambda e: e.activation(out=o, in_=bk.ap, func=AF.Copy, scale=QSCALE), reads=[bk], writes=[sg], nowaw=True)
            else:
                B.op("dve", lambda e: e.tensor_scalar(out=o, in0=bk.ap, scalar1=QSCALE, scalar2=None, op0=ALU.mult),
                     reads=[bk], writes=[sg], nowaw=True)
        else:
            if par == 0:
                B.op("act", lambda e: e.activation(out=o, in_=bk.ap, func=AF.Copy), reads=[bk], writes=[sg], nowaw=True)
            else:
                B.op("dve", lambda e: e.tensor_copy(out=o, in_=bk.ap), reads=[bk], writes=[sg], nowaw=True)

    def mm_T(bk, wt, m_, hT, t0, t1, c0):
        return [lambda e, kc=kc: e.matmul(bk.ap[:, c0:c0 + (t1 - t0)], wt.ap[:, kc, m_ * 128:(m_ + 1) * 128], hT.ap[:, kc, t0:t1],
                                          start=(kc == 0), stop=(kc == 15)) for kc in range(16)]

    def proj_T(hT, wt, m_, dsts, kind, halo=False):
        sg = stg[ps["si"] % 2]
        dsm = DS[6 + ps["si"] % 2]
        ps["si"] += 1
        if not halo:
            bks = [banks[(ps["bi"] + tg) % 8] for tg in range(4)]
            ps["bi"] += 4
            fns = []
            for kc in range(16):
                for tg in range(4):
                    fns.append(lambda e, bk=bks[tg], kc=kc, tg=tg: e.matmul(
                        bk.ap, wt.ap[:, kc, m_ * 128:(m_ + 1) * 128], hT.ap[:, kc, tg * 512:(tg + 1) * 512],
                        start=(kc == 0), stop=(kc == 15)))
            B.group("pe", fns, reads=[wt, hT], writes=bks)
            for tg in range(4):
                evac(kind, sg.ap[:, tg, :], bks[tg], sg, tg % 2)
        else:
            bk = banks[ps["bi"] % 8]
            ps["bi"] += 1
            fns = mm_T(bk, wt, m_, hT, T - 256, T, 0) + mm_T(bk, wt, m_, hT, 0, 256, 256)
            B.group("pe", fns, reads=[wt, hT], writes=[bk])
            evac(kind, sg.ap[:, 0, :], bk, sg, 0)
        for (dst, c0, c1, dslot) in dsts:
            B.dma("sp", dst, sg.ap.rearrange("p a b -> p (a b)")[:, c0:c1], dsm, reads=[sg], writes=[dslot])

    def proj_V(hT, wt, c0, dst, dslot, tts, d0):
        sg = stg[ps["si"] % 2]
        dsm = DS[6 + ps["si"] % 2]
        ps["si"] += 1
        for ii, tt in enumerate(tts):
            bk = banks[ps["bi"] % 8]
            ps["bi"] += 1
            fns = [lambda e, bk=bk, kc=kc, tt=tt: e.matmul(
                bk.ap, hT.ap[:, kc, tt * 128:(tt + 1) * 128], wt.ap[:, kc, :], start=(kc == 0), stop=(kc == 15))
                for kc in range(16)]
            B.group("pe", fns, reads=[wt, hT], writes=[bk])
            evac("k", sg.ap[:, ii, :], bk, sg, ii % 2)
        n = len(tts)
        B.dma("sp", dst[d0 * 128:(d0 + n) * 128, c0:c0 + 512].rearrange("(a p) n -> p a n", p=128),
              sg.ap[:, 0:n, :], dsm, reads=[sg], writes=[dslot])

    stg = []

    def phase_proj(j, xsrc, js, full=True):
        m = AR.off
        hT = AR.slot([16, T], BF16)
        phase_A(xsrc, js, hT)
        if os.environ.get("KSTOP") == "A":
            return
        wtiles = [AR.slot([16, 512], BF16) for _ in range(3)]
        stg.clear()
        stg.extend([AR.slot([4, 512], BF16) for _ in range(2)])
        order = list(range(20)) if full else [2, 3, 4, 5, 8, 9, 10, 11]
        wstream[0] = WStream(wtiles, [DS[4], DS[5], DS[8]], [(wb_in[nb], W_in[nb // 5], 16) for nb in order])
        if full:
            knoff = 256 if j == 0 else 0
            for nb in range(20):
                wt = load_w(wtiles, nb)
                if nb in (4, 5):
                    for g4 in range(4):
                        proj_V(hT, wt, (nb % 2) * 512, va[j], S_va[j], [g4 * 4 + i for i in range(4)], g4 * 4)
                elif nb in (10, 11):
                    for g4 in range(4):
                        proj_V(hT, wt, (nb % 2) * 512, vn[j], S_vn[j], [g4 * 4 + i for i in range(4)], g4 * 4 + knoff // 128)
                else:
                    for m_ in range(4):
                        r0 = (nb % 2) * 512 + m_ * 128
                        if nb in (0, 1):
                            d, sl, kind = qaT[j][r0:r0 + 128, :], S_qaT[j], "q"
                        elif nb in (2, 3):
                            d, sl, kind = kaT[j][r0:r0 + 128, 0:T], S_kaT[j], "k"
                        elif nb in (6, 7):
                            d, sl, kind = qnT[j][r0:r0 + 128, :], S_qnT[j], "q"
                        elif nb in (8, 9):
                            d, sl, kind = knT[j][r0:r0 + 128, knoff:knoff + T], S_knT[j], "k"
                        else:
                            r0g = (nb - 12) * 512 + m_ * 128
                            d, sl, kind = gT[j][r0g:r0g + 128, :], S_gT[j], "sig"
                        proj_T(hT, wt, m_, [(d, 0, 2048, sl)], kind)
        else:
            def pbank():
                b_ = banks[ps["bi"] % 8]
                ps["bi"] += 1
                return b_
            for nb in (2, 3):
                wt = load_w(wtiles, nb)
                for m_ in range(4):
                    r0 = (nb % 2) * 512 + m_ * 128
                    proj_T(hT, wt, m_, [(kaT[0][r0:r0 + 128, T:2 * T], 0, 2048, S_kaT[0])], "k")
                    ada_half_tiles(2, pbank)
            conv_rest((B.prog["pe"], B.prog["pe"].n, "pe"))
            for nb in (4, 5):
                wt = load_w(wtiles, nb)
                for g4 in range(4):
                    proj_V(hT, wt, (nb % 2) * 512, va[0], S_va[0], [g4 * 4 + i for i in range(4)], 16 + g4 * 4)
                    ada_half_tiles(2, pbank)
            ada_half_tiles(48, pbank)
            ada_finish(32, 96, [(s2p, 64, g_mlpT)])
            for nb in (8, 9):
                wt = load_w(wtiles, nb)
                for m_ in range(4):
                    r0 = (nb % 2) * 512 + m_ * 128
                    proj_T(hT, wt, m_, [(knT[0][r0:r0 + 128, 0:256], 0, 256, S_knT[0]),
                                        (knT[0][r0:r0 + 128, 2304:2560], 256, 512, S_knT[0])], "k", halo=True)
            for nb in (10, 11):
                wt = load_w(wtiles, nb)
                proj_V(hT, wt, (nb % 2) * 512, vn[0], S_vn[0], [14, 15], 0)
                proj_V(hT, wt, (nb % 2) * 512, vn[0], S_vn[0], [0, 1], 18)
        B.barrier()
        AR.off = m

    def phase_DA(j):
        m = AR.off
        Sk = SK[j]
        nkt = Sk // 128
        dcon = AR.slot([5, 512], F32)
        tabc = AR.slot([132], F32)
        tabo = AR.slot([260], F32)
        slp = AR.slot([8], F32)
        B.dma("sp", dcon.ap.rearrange("p a b -> p (a b)"), dconst_d, DS[8], writes=[dcon])
        B.dma("sp", tabc.ap, tabc_d, DS[8], writes=[tabc])
        B.dma("sp", tabo.ap, tabo_d, DS[8], writes=[tabo])
        B.dma("sp", slp.ap, slp_d, DS[8], writes=[slp])
        B.batch_fix(DS[8], [dcon, tabc, tabo, slp])
        qT_h = [AR.slot([2, T], BF16) for _ in range(2)]
        kT_h = [AR.slot([2, Sk], BF16) for _ in range(2)]
        V_h = [AR.slot([nkt, 257], BF16) for _ in range(2)]
        for v_ in V_h:
            B.op("dve", lambda e, v_=v_: e.memset(v_.ap[:, :, 256:257], 1.0), writes=[v_])
        NBUF = 4
        LOOK = 3
        Sb = [AR.slot([512], F32) for _ in range(NBUF)]
        PT = [AR.slot([512], BF16) for _ in range(NBUF)]
        Oc = [[AR.slot([257], F32) for _ in range(4)] for _ in range(2)]
        tmpq = [AR.slot([256], F32) for _ in range(4)]
        oaf = [AR.slot([256], F32) for _ in range(4)]
        stq = [AR.slot([8], F32) for _ in range(4)]
        junk = AR.slot([256], BF16)
        ostg = [AR.slot([4, 256], BF16) for _ in range(2)]
        Ob = banks[0:4]
        Sbk = banks[4:8]
        cnt = [0]

        def emit_S(q_, k_, c, g, kt):
            bk = Sbk[cnt[0] % 4]
            B.group("pe", [lambda e: e.matmul(bk.ap, k_.ap[:, c, kt * 128:(kt + 1) * 128], q_.ap[:, c, g * 512:(g + 1) * 512],
                                              start=True, stop=True)], reads=[k_, q_], writes=[bk])
            r = (cnt[0], bk)
            cnt[0] += 1
            return r

        def emit_rest(v_, h, g, kt, step_, bk):
            if j == 0 and kt >= 16:
                a_ap = tabo.ap[:, h:h + 1]
                ci = 4 + h * 64 + g * 16 + (kt - 16)
                b_ap = tabo.ap[:, ci:ci + 1]
                dsel = 0
                rd = [tabo]
            else:
                delta = g * 512 - kt * 128
                if delta >= 128:
                    a_ap = slp.ap[:, 4 + h:5 + h]
                    mi = delta // 128
                    dsel = 0
                elif delta <= -512:
                    a_ap = slp.ap[:, h:h + 1]
                    mi = (-delta) // 128
                    dsel = 0
                else:
                    a_ap = slp.ap[:, 4 + h:5 + h]
                    mi = 0
                    dsel = 1 + (-delta) // 128
                b_ap = tabc.ap[:, h * 33 + mi:h * 33 + mi + 1]
                rd = [tabc, slp]
            sb = Sb[step_ % NBUF]
            pt = PT[step_ % NBUF]
            B.op("dve", lambda e: e.scalar_tensor_tensor(out=sb.ap, in0=dcon.ap[:, dsel, :], scalar=a_ap, in1=bk.ap,
                                                         op0=ALU.mult, op1=ALU.add), reads=[bk, dcon] + rd, writes=[sb])
            B.op("act", lambda e: e.activation(out=pt.ap, in_=sb.ap, func=AF.Exp, bias=b_ap, scale=1.0),
                 reads=[sb] + rd, writes=[pt])
            fns = [lambda e, qs=qs: e.matmul(Ob[qs].ap[:, 0:257], pt.ap[:, qs * 128:(qs + 1) * 128], v_.ap[:, kt, :],
                                             start=(kt == 0), stop=(kt == nkt - 1)) for qs in range(4)]
            B.group("pe", fns, reads=[pt, v_], writes=Ob if kt == 0 else [], cont=Ob if kt > 0 else [])

        deferred = []

        def flush(n=None):
            k = len(deferred) if n is None else min(n, len(deferred))
            for _ in range(k):
                deferred.pop(0)()

        def evac_O(h, g, c, og):
            if c == 0:
                flush()
            for qs in range(4):
                ob = Ob[qs]
                oc = Oc[c][qs]
                if qs % 2 == 0:
                    B.op("act", lambda e, ob=ob, oc=oc: e.activation(out=oc.ap, in_=ob.ap[:, 0:257], func=AF.Copy),
                         reads=[ob], writes=[oc])
                else:
                    B.op("dve", lambda e, ob=ob, oc=oc: e.tensor_copy(out=oc.ap, in_=ob.ap[:, 0:257]), reads=[ob], writes=[oc])
            if c == 0:
                return
            flush()
            stages = [[] for _ in range(10)]
            for qs in range(4):
                o0, o1, sq_, tq, of = Oc[0][qs], Oc[1][qs], stq[qs], tmpq[qs], oaf[qs]
                stages[0].append(lambda sq_=sq_: B.op("pool", lambda e: e.memset(sq_.ap, 0.0), writes=[sq_]))
                stages[1].append(lambda o0=o0, sq_=sq_: B.op("dve", lambda e: e.reciprocal(out=sq_.ap[:, 0:1], in_=o0.ap[:, 256:257]),
                                                             reads=[o0], writes=[sq_]))
                stages[2].append(lambda o1=o1, sq_=sq_: B.op("dve", lambda e: e.reciprocal(out=sq_.ap[:, 1:2], in_=o1.ap[:, 256:257]),
                                                             reads=[o1], writes=[sq_]))
                stages[3].append(lambda sq_=sq_: B.op("dve", lambda e: e.tensor_tensor(out=sq_.ap[:, 2:3], in0=sq_.ap[:, 1:2], in1=neglam.ap,
                                                                                       op=ALU.mult), reads=[sq_, neglam], writes=[sq_]))
                stages[4].append(lambda o0=o0, tq=tq, sq_=sq_: B.op("act", lambda e: e.activation(
                    out=tq.ap, in_=o0.ap[:, 0:256], func=AF.Copy, scale=sq_.ap[:, 0:1]), reads=[o0, sq_], writes=[tq]))
                stages[5].append(lambda o1=o1, tq=tq, of=of, sq_=sq_: B.op("dve", lambda e: e.scalar_tensor_tensor(
                    out=of.ap, in0=o1.ap[:, 0:256], scalar=sq_.ap[:, 2:3], in1=tq.ap, op0=ALU.mult, op1=ALU.add),
                    reads=[o1, tq, sq_], writes=[of]))
                stages[6].append(lambda of=of, sq_=sq_: B.op("act", lambda e: e.activation(
                    out=junk.ap, in_=of.ap, func=AF.Square, accum_out=sq_.ap[:, 3:4]), reads=[of, sq_], writes=[junk, sq_]))
                stages[7].append(lambda sq_=sq_: B.op("act", lambda e: e.activation(
                    out=sq_.ap[:, 4:5], in_=sq_.ap[:, 3:4], func=AF.Ln, bias=epsc.ap, scale=1.0 / 256), reads=[sq_, epsc], writes=[sq_]))
                stages[8].append(lambda sq_=sq_: B.op("act", lambda e: e.activation(
                    out=sq_.ap[:, 5:6], in_=sq_.ap[:, 4:5], func=AF.Exp, scale=-0.5), reads=[sq_], writes=[sq_]))
                stages[9].append(lambda qs=qs, of=of, sq_=sq_: B.op("dve", lambda e: e.scalar_tensor_tensor(
                    out=og.ap[:, qs, :], in0=of.ap, scalar=sq_.ap[:, 5:6], in1=gsub_bc.ap, op0=ALU.mult, op1=ALU.mult),
                    reads=[of, sq_, gsub_bc], writes=[og], nowaw=True))
            for st_ in stages:
                deferred.extend(st_)
            deferred.append(lambda: B.dma(
                "sp", oa_s[j][g * 512:(g + 1) * 512, h * 256:(h + 1) * 256].rearrange("(a p) n -> p a n", p=128),
                og.ap, DS[11 + (h * 4 + g) % 2], reads=[og], writes=[S_oa[j]]))

        def load_head(h):
            q_ = qT_h[h % 2]
            k_ = kT_h[h % 2]
            v_ = V_h[h % 2]
            dsm = DS[9 + h % 2]
            B.dma("sp", q_.ap, qaT[j][h * 256:(h + 1) * 256, :].rearrange("(c p) t -> p c t", p=128), dsm,
                  reads=[S_qaT[j]], writes=[q_])
            B.dma("sp", k_.ap, kaT[j][h * 256:(h + 1) * 256, :].rearrange("(c p) t -> p c t", p=128), dsm,
                  reads=[S_kaT[j]], writes=[k_])
            B.dma("sp", v_.ap[:, :, 0:256], va[j][:, h * 256:(h + 1) * 256].rearrange("(a p) n -> p a n", p=128),
                  dsm, reads=[S_va[j]], writes=[v_])
            B.batch_fix(dsm, [q_, k_, v_])

        load_head(0)
        load_head(1)
        pend = []

        def retire():
            a = pend.pop(0)
            emit_rest(*a[:6])
            flush(2)
            if a[3] == nkt - 1:
                evac_O(a[1], a[2], a[6], ostg[(a[1] * 4 + a[2]) % 2])
                if a[2] == 3 and a[6] == 1 and a[1] + 2 < 4:
                    load_head(a[1] + 2)

        for h in range(4):
            q_ = qT_h[h % 2]
            k_ = kT_h[h % 2]
            v_ = V_h[h % 2]
            for g in range(4):
                for c in range(2):
                    for kt in range(nkt):
                        st_, bk = emit_S(q_, k_, c, g, kt)
                        pend.append((v_, h, g, kt, st_, bk, c))
                        if len(pend) > LOOK:
                            retire()
        while pend:
            retire()
        flush()
        B.barrier()
        AR.off = m

    def phase_NA(j):
        m = AR.off
        jt = "p" if j == 0 else "s"
        keys, order, rep, offs, ntile = na_groups(jt)
        nab_d = nab_p_d if j == 0 else nab_s_d
        Skn = SKN[j]
        nkt = Skn // 128
        pad = 2 if j == 0 else 0
        q_h = [AR.slot([T], BF16) for _ in range(2)]
        k_h = [AR.slot([Skn], BF16) for _ in range(2)]
        v_h = [AR.slot([nkt, 129], BF16) for _ in range(2)]
        nb_h = [AR.slot([ntile, 128], F32) for _ in range(2)]
        for v_ in v_h:
            B.op("pool", lambda e, v_=v_: e.memset(v_.ap[:, :, 128:129], 1.0), writes=[v_])
        Sb = [AR.slot([6, 128], F32) for _ in range(3)]
        PT = [AR.slot([6, 128], BF16) for _ in range(3)]
        rl = AR.slot([16, 2], F32)
        ostg = [AR.slot([16, 128], BF16) for _ in range(2)]
        it = [0]

        def emit_S(q_, k_, jq):
            it_ = it[0]
            it[0] += 1
            rel = na_rel(jt, jq)
            n = len(rel)
            bA = banks[(it_ % 3) * 2]
            bB = banks[(it_ % 3) * 2 + 1]
            fns = []
            for i, r in enumerate(rel):
                kt = jq + r + pad
                tgt = bA.ap[:, i * 128:(i + 1) * 128] if i < 4 else bB.ap[:, (i - 4) * 128:(i - 3) * 128]
                fns.append(lambda e, tgt=tgt, kt=kt: e.matmul(
                    tgt, k_.ap[:, kt * 128:(kt + 1) * 128], q_.ap[:, jq * 128:(jq + 1) * 128], start=True, stop=True))
            B.group("pe", fns, reads=[k_, q_], writes=[bA, bB] if n > 4 else [bA])
            return (jq, it_, bA, bB, rel)

        def emit_rest(v_, nb_, og, jq, it_, bA, bB, rel):
            n = len(rel)
            off = offs[keys.get(jq, "int")]
            sb = Sb[it_ % 3]
            pt = PT[it_ % 3]
            na_ = min(n, 4)
            B.op("dve", lambda e: e.tensor_tensor(
                out=sb.ap[:, 0:na_, :], in0=bA.ap[:, 0:na_ * 128].rearrange("p (a b) -> p a b", a=na_, b=128),
                in1=nb_.ap[:, off:off + na_, :], op=ALU.add), reads=[bA, nb_], writes=[sb])
            if n > 4:
                nbb = n - 4
                B.op("dve", lambda e: e.tensor_tensor(
                    out=sb.ap[:, 4:4 + nbb, :], in0=bB.ap[:, 0:nbb * 128].rearrange("p (a b) -> p a b", a=nbb, b=128),
                    in1=nb_.ap[:, off + 4:off + 4 + nbb, :], op=ALU.add), reads=[bB, nb_], writes=[sb])
            B.op("act", lambda e: e.activation(out=pt.ap[:, 0:n, :], in_=sb.ap[:, 0:n, :], func=AF.Exp),
                 reads=[sb], writes=[pt])
            ob = banks[6 + it_ % 2]
            fns = []
            for i, r in enumerate(rel):
                kt = jq + r + pad
                fns.append(lambda e, i=i, kt=kt: e.matmul(
                    ob.ap[:, 0:129], pt.ap[:, i, :], v_.ap[:, kt, :], start=(i == 0), stop=(i == n - 1)))
            B.group("pe", fns, reads=[pt, v_], writes=[ob])
            B.op("dve", lambda e: e.reciprocal(out=rl.ap[:, jq, 0:1], in_=ob.ap[:, 128:129]), reads=[ob], writes=[rl])
            B.op("act", lambda e: e.activation(out=og.ap[:, jq, :], in_=ob.ap[:, 0:128], func=AF.Copy, scale=rl.ap[:, jq, 0:1]),
                 reads=[ob, rl], writes=[og])

        def load_head(h):
            q_ = q_h[h % 2]
            k_ = k_h[h % 2]
            v_ = v_h[h % 2]
            nb_ = nb_h[h % 2]
            ds_ = DS[13 + h % 2]
            B.dma("sp", q_.ap, qnT[j][h * 128:(h + 1) * 128, :], ds_, reads=[S_qnT[j]], writes=[q_])
            B.dma("sp", k_.ap, knT[j][h * 128:(h + 1) * 128, :], ds_, reads=[S_knT[j]], writes=[k_])
            B.dma("sp", nb_.ap.rearrange("p a b -> p (a b)"), nab_d[h], ds_, writes=[nb_])
            B.dma("sp", v_.ap[:, :, 0:128], vn[j][:, h * 128:(h + 1) * 128].rearrange("(a p) n -> p a n", p=128),
                  ds_, reads=[S_vn[j]], writes=[v_])
            B.batch_fix(ds_, [q_, k_, nb_, v_])

        load_head(0)
        for h in range(8):
            if h + 1 < 8:
                load_head(h + 1)
            q_ = q_h[h % 2]
            k_ = k_h[h % 2]
            v_ = v_h[h % 2]
            nb_ = nb_h[h % 2]
            og = ostg[h % 2]
            pend = []
            for jq in range(16):
                pend.append(emit_S(q_, k_, jq))
                if len(pend) > 2:
                    emit_rest(v_, nb_, og, *pend.pop(0))
            while pend:
                emit_rest(v_, nb_, og, *pend.pop(0))
            B.dma("sp", on_s[j][:, h * 128:(h + 1) * 128].rearrange("(a p) n -> p a n", p=128), og.ap,
                  DS[15 + h % 2], reads=[og], writes=[S_on[j]])
        B.barrier()
        AR.off = m

    def phase_D(j, js):
        m = AR.off
        bufA = AR.slot([16, 512], BF16)
        bufB = AR.slot([16, 512], BF16)
        hid = AR.slot([16, 512], BF16)
        x1 = [AR.slot([D], F32) for _ in range(4)]
        xn2 = AR.slot([D], BF16)
        tokin = [AR.slot([1024], BF16) for _ in range(2)]
        sgt = [AR.slot([2, 512], BF16) for _ in range(2)]
        t1 = AR.slot([512], F32)
        t2 = AR.slot([512], F32)
        rr = [AR.slot([512], F32) for _ in range(2)]
        tmpa = [AR.slot([512], F32) for _ in range(2)]
        wts = [AR.slot([16, 512], BF16) for _ in range(3)]
        st = AR.slot([4, 8], F32)
        junk = AR.slot([D], BF16)
        wi = [0]
        bi = [0]

        def nbank():
            b = banks[bi[0] % 8]
            bi[0] += 1
            return b

        dspecs = []
        for _tg in range(4):
            for nb in range(4):
                dspecs.append((wb_pa[nb], W_pa, 8))
                dspecs.append((wb_pb[nb], W_pb, 8))
            for cg in range(4):
                dspecs.append((wb_out[cg], W_out, 16))
            for qq in range(4):
                for nb4 in range(4):
                    dspecs.append((wb_1[qq * 4 + nb4], W_1, 16))
                for cg in range(4):
                    dspecs.append((wb_2[qq * 4 + cg], W_2, 16))
        dstream = WStream(wts, [DS[4], DS[5], DS[6]], dspecs)

        def wload(src, rd, nk=16):
            return dstream.get()

        def transposes_in(tg, si, src, sl):
            t0 = tg * 512
            tb = [nbank() for _ in range(4)]
            for tt in range(4):
                k = si * 4 + tt
                ti = tokin[k % 2]
                B.dma("sp", ti.ap, src[t0 + tt * 128:t0 + (tt + 1) * 128, :], DS[7 + k % 2], reads=[sl], writes=[ti])
                fns = []
                for fc in range(8):
                    pv = tb[fc // 2].ap.bitcast(BF16)
                    c0 = (fc % 2) * 512 + tt * 128
                    fns.append(lambda e, pv=pv, fc=fc, c0=c0, ti=ti: e.transpose(
                        pv[:, c0:c0 + 128], ti.ap[:, fc * 128:(fc + 1) * 128], ident.ap))
                B.group("pe", fns, reads=[ti, ident], writes=tb if tt == 0 else [], cont=tb if tt > 0 else [])
            for b2 in range(4):
                pv = tb[b2].ap.bitcast(BF16).rearrange("p (a b) -> p a b", a=2, b=512)
                dst = bufA.ap[:, si * 8 + b2 * 2:si * 8 + b2 * 2 + 2, :]
                if False:
                    B.op("act", lambda e, pv=pv, dst=dst: e.activation(out=dst, in_=pv, func=AF.Copy),
                         reads=[tb[b2]], writes=[bufA], nowaw=True)
                else:
                    B.op("dve", lambda e, pv=pv, dst=dst: e.tensor_copy(out=dst, in_=pv), reads=[tb[b2]], writes=[bufA], nowaw=True)

        def rms_stats(tt, c0):
            B.op("act", lambda e: e.activation(out=junk.ap, in_=x1[tt].ap, func=AF.Square, accum_out=st.ap[:, tt, c0:c0 + 1]),
                 reads=[x1[tt], st], writes=[junk, st])
            B.op("act", lambda e: e.activation(out=st.ap[:, tt, c0 + 1:c0 + 2], in_=st.ap[:, tt, c0:c0 + 1], func=AF.Ln,
                                               bias=epsc.ap, scale=1.0 / D), reads=[st, epsc], writes=[st])
            B.op("act", lambda e: e.activation(out=st.ap[:, tt, c0 + 2:c0 + 3], in_=st.ap[:, tt, c0 + 1:c0 + 2], func=AF.Exp, scale=-0.5),
                 reads=[st], writes=[st])

        def accum(bk, tt, cg, gbc, k):
            ta = tmpa[k % 2]
            B.op("dve", lambda e: e.tensor_tensor(out=ta.ap, in0=bk.ap, in1=gbc.ap[:, cg * 512:(cg + 1) * 512], op=ALU.mult),
                 reads=[bk, gbc], writes=[ta])
            B.op("pool", lambda e: e.tensor_tensor(out=x1[tt].ap[:, cg * 512:(cg + 1) * 512],
                                                   in0=x1[tt].ap[:, cg * 512:(cg + 1) * 512], in1=ta.ap, op=ALU.add),
                 reads=[ta, x1[tt]], writes=[x1[tt]])

        for tg in range(4):
            t0 = tg * 512
            B.op("pool", lambda e: e.memset(st.ap, 0.0), reads=[st], writes=[st])
            transposes_in(tg, 0, oa_s[j], S_oa[j])
            transposes_in(tg, 1, on_s[j], S_on[j])
            for nb in range(4):
                wa = wload(wb_pa[nb], W_pa, nk=8)
                wb_ = wload(wb_pb[nb], W_pb, nk=8)
                for mm in range(4):
                    mc = nb * 4 + mm
                    sg_ = sgt[mc % 2]
                    dsm = DS[9 + mc % 2]
                    B.dma("sp", sg_.ap[:, 0, :], gT[j][mc * 128:(mc + 1) * 128, t0:t0 + 512], dsm, reads=[S_gT[j]], writes=[sg_])
                    B.dma("sp", sg_.ap[:, 1, :], gT[j][2048 + mc * 128:2048 + (mc + 1) * 128, t0:t0 + 512], dsm,
                          reads=[S_gT[j]], writes=[sg_])
                    ba = nbank()
                    bb = nbank()
                    B.group("pe", [lambda e, ba=ba, wa=wa, mm=mm, kc=kc: e.matmul(
                        ba.ap, wa.ap[:, kc, mm * 128:(mm + 1) * 128], bufA.ap[:, kc, :], start=(kc == 0), stop=(kc == 7))
                        for kc in range(8)], reads=[wa, bufA], writes=[ba])
                    B.group("pe", [lambda e, bb=bb, wb_=wb_, mm=mm, kc=kc: e.matmul(
                        bb.ap, wb_.ap[:, kc, mm * 128:(mm + 1) * 128], bufA.ap[:, 8 + kc, :], start=(kc == 0), stop=(kc == 7))
                        for kc in range(8)], reads=[wb_, bufA], writes=[bb])
                    B.op("dve", lambda e, ba=ba, sg_=sg_: e.tensor_tensor(out=t1.ap, in0=ba.ap, in1=sg_.ap[:, 0, :], op=ALU.mult),
                         reads=[ba, sg_], writes=[t1])
                    B.op("dve", lambda e, bb=bb, sg_=sg_: e.tensor_tensor(out=t2.ap, in0=bb.ap, in1=sg_.ap[:, 1, :], op=ALU.mult),
                         reads=[bb, sg_], writes=[t2])
                    B.op("pool", lambda e, mc=mc: e.tensor_tensor(out=bufB.ap[:, mc, :], in0=t1.ap, in1=t2.ap, op=ALU.add),
                         reads=[t1, t2], writes=[bufB])
            for tt in range(4):
                B.dma("sp", x1[tt].ap, xq[j, t0 + tt * 128:t0 + (tt + 1) * 128, :], DS[16 + tt], writes=[x1[tt]])
            k = 0
            for cg in range(4):
                wo = wload(wb_out[cg], W_out)
                for tt in range(4):
                    bk = nbank()
                    B.group("pe", [lambda e, bk=bk, wo=wo, tt=tt, kc=kc: e.matmul(
                        bk.ap, bufB.ap[:, kc, tt * 128:(tt + 1) * 128], wo.ap[:, kc, :], start=(kc == 0), stop=(kc == 15))
                        for kc in range(16)], reads=[wo, bufB], writes=[bk])
                    accum(bk, tt, cg, gt1_bc, k)
                    k += 1
            for tt in range(4):
                rms_stats(tt, 0)
                B.op("act", lambda e, tt=tt: e.activation(out=xn2.ap, in_=x1[tt].ap, func=AF.Copy, scale=st.ap[:, tt, 2:3]),
                     reads=[x1[tt], st], writes=[xn2])
                for k4 in range(4):
                    bk = nbank()
                    pv = bk.ap.bitcast(BF16)
                    B.group("pe", [lambda e, pv=pv, kk=kk, k4=k4: e.transpose(
                        pv[:, kk * 128:(kk + 1) * 128], xn2.ap[:, (k4 * 4 + kk) * 128:(k4 * 4 + kk + 1) * 128], ident.ap)
                        for kk in range(4)], reads=[xn2, ident], writes=[bk])
                    for kk in range(4):
                        kc = k4 * 4 + kk
                        dst = bufA.ap[:, kc, tt * 128:(tt + 1) * 128]
                        if False:
                            B.op("act", lambda e, pv=pv, kk=kk, kc=kc, dst=dst: e.activation(
                                out=dst, in_=pv[:, kk * 128:(kk + 1) * 128], func=AF.Identity,
                                bias=adaT.ap[:, 48 + kc, js:js + 1], scale=s2p.ap[:, kc, js:js + 1]),
                                reads=[bk, adaT, s2p], writes=[bufA], nowaw=True)
                        else:
                            B.op("dve", lambda e, pv=pv, kk=kk, kc=kc, dst=dst: e.tensor_scalar(
                                out=dst, in0=pv[:, kk * 128:(kk + 1) * 128], scalar1=s2p.ap[:, kc, js:js + 1],
                                scalar2=adaT.ap[:, 48 + kc, js:js + 1], op0=ALU.mult, op1=ALU.add),
                                reads=[bk, adaT, s2p], writes=[bufA], nowaw=True)
            for qq in range(4):
                for nb4 in range(4):
                    w1t = wload(wb_1[qq * 4 + nb4], W_1)
                    for mm in range(4):
                        fc = nb4 * 4 + mm
                        bk = nbank()
                        B.group("pe", [lambda e, bk=bk, w1t=w1t, mm=mm, kc=kc: e.matmul(
                            bk.ap, w1t.ap[:, kc, mm * 128:(mm + 1) * 128], bufA.ap[:, kc, :], start=(kc == 0), stop=(kc == 15))
                            for kc in range(16)], reads=[w1t, bufA], writes=[bk])
                        r_ = rr[fc % 2]
                        B.op("act", lambda e, bk=bk, r_=r_: e.activation(out=r_.ap, in_=bk.ap, func=AF.Relu), reads=[bk], writes=[r_])
                        B.op("pool", lambda e, r_=r_, fc=fc: e.tensor_tensor(out=hid.ap[:, fc, :], in0=r_.ap, in1=r_.ap, op=ALU.mult),
                             reads=[r_], writes=[hid])
                k = 0
                for cg in range(4):
                    w2t = wload(wb_2[qq * 4 + cg], W_2)
                    for tt in range(4):
                        bk = nbank()
                        B.group("pe", [lambda e, bk=bk, w2t=w2t, tt=tt, fc=fc: e.matmul(
                            bk.ap, hid.ap[:, fc, tt * 128:(tt + 1) * 128], w2t.ap[:, fc, :], start=(fc == 0), stop=(fc == 15))
                            for fc in range(16)], reads=[w2t, hid], writes=[bk])
                        accum(bk, tt, cg, gt2_bc, k)
                        k += 1
            for tt in range(4):
                rms_stats(tt, 3)
                B.op("dve", lambda e, tt=tt: e.scalar_tensor_tensor(
                    out=x1[tt].ap, in0=x1[tt].ap, scalar=st.ap[:, tt, 5:6], in1=gf_bc.ap, op0=ALU.mult, op1=ALU.mult),
                    reads=[x1[tt], st, gf_bc], writes=[x1[tt]])
                B.dma("sp", y[j, t0 + tt * 128:t0 + (tt + 1) * 128, :], x1[tt].ap, DS[20 + tt], reads=[x1[tt]], writes=[S_y])
        B.barrier()
        AR.off = m

    if stage >= 1:
        phase_proj(0, xo, 0, full=False)
        AR.n = ARENA_ELEMS
        if os.environ.get("KSTOP") is None:
            phase_proj(0, xq[0], 0, full=True)
    if stage >= 2:
        phase_DA(0)
    if stage >= 3:
        phase_NA(0)
    if stage >= 4:
        make_gt_bc(0)
        phase_D(0, 0)
    if stage >= 5:
        for j in (1, 2):
            phase_proj(j, xq[j], j, full=True)
            phase_DA(j)
            phase_NA(j)
            make_gt_bc(j)
            phase_D(j, j)
    B.barrier()

    with nc.Block() as block:
        B.emit(block)
    es.close()
    return nc, B


def make_in_maps(inputs):
    f = lambda a: np.ascontiguousarray(np.asarray(a, dtype=np.float32))
    x_prompt = f(inputs["x_prompt"])
    x_sample = f(inputs["x_sample"])
    c_prompt = f(inputs["c_prompt"])
    c_sample = f(inputs["c_sample"])
    rpb = f(inputs["rpb"])[0]
    rpb_ext = np.concatenate([rpb.reshape(8, 465), np.full((8, 1), NEG, np.float32)], axis=1)
    common = {
        "w_ada": f(inputs["w_ada"])[0],
        "b_adaT": f(f(inputs["b_ada"])[0].reshape(96, 128).T),
        "g_mixT": f(f(inputs["g_mix"])[0].reshape(16, 128).T),
        "g_mlpT": f(f(inputs["g_mlp"])[0].reshape(16, 128).T),
        "g_final": f(inputs["g_final"]).reshape(1, D),
        "g_subln": f(inputs["g_subln"]).reshape(1, 256),
        "lam4": f(np.stack([f(inputs["lam_q1"])[0], f(inputs["lam_k1"])[0], f(inputs["lam_q2"])[0], f(inputs["lam_k2"])[0]], 0)),
        "w_in": f(inputs["w_in"])[0],
        "w_pa": f(inputs["w_pa"])[0],
        "w_pb": f(inputs["w_pb"])[0],
        "w_out": f(inputs["w_out"])[0],
        "w1": f(inputs["w1"])[0],
        "w2": f(inputs["w2"])[0],
    }
    idx_s = build_na_idx("s", 0)
    idx_s = np.where(idx_s < 0, 465, idx_s)
    nab_s = f(rpb_ext[:, idx_s].transpose(0, 2, 1, 3).reshape(8, 128, 21 * 128))
    nab_p = []
    tabs = []
    for half in range(2):
        idx_p = build_na_idx("p", half)
        idx_p = np.where(idx_p < 0, 465, idx_p)
        nab_p.append(f(rpb_ext[:, idx_p].transpose(0, 2, 1, 3).reshape(8, 128, 27 * 128)))
        tabs.append(build_tables(half))
    maps = []
    for c in range(8):
        b, half = c // 2, c % 2
        xq = np.stack([x_prompt[b, half * T:(half + 1) * T], x_sample[2 * c], x_sample[2 * c + 1]], 0)
        xo = x_prompt[b, (1 - half) * T:(2 - half) * T]
        cs = np.stack([c_prompt[b], c_sample[2 * c], c_sample[2 * c + 1], np.zeros(D, np.float32)], 0)
        cT = f(cs.reshape(4, 16, 128).transpose(2, 1, 0).reshape(128, 64))
        dconst, tabc, tabo, slp = tabs[half]
        mp = dict(common)
        mp.update({"xq": f(xq), "xo": f(xo), "cT": cT, "nab_s": nab_s, "nab_p": nab_p[half],
                   "dconst": f(dconst.reshape(128, 2560)), "tabc": tabc, "tabo": tabo, "slp": slp})
        maps.append(mp)
    return maps


_NC_CACHE = {}


def kernel(**inputs):
    if "nc" not in _NC_CACHE:
        _NC_CACHE["nc"] = build_nc()[0]
    nc = _NC_CACHE["nc"]
    maps = make_in_maps(inputs)
    res = run_bass_kernel_spmd(nc, maps, core_ids=list(range(8)))
    y_prompt = np.zeros((4, 4096, D), np.float32)
    y_sample = np.zeros((16, T, D), np.float32)
    for c in range(8):
        yy = res.results[c]["y"]
        y_prompt[c // 2, (c % 2) * T:(c % 2 + 1) * T] = yy[0]
        y_sample[2 * c] = yy[1]
        y_sample[2 * c + 1] = yy[2]
    return (y_prompt, y_sample)
```

```python
import math
import os
from contextlib import ExitStack

import numpy as np
import concourse.bass as bass
import concourse.mybir as mybir
from concourse.bass_utils import run_bass_kernel_spmd

F32 = mybir.dt.float32
BF16 = mybir.dt.bfloat16
AF = mybir.ActivationFunctionType
ALU = mybir.AluOpType

D = 2048
NJOB = 3
T = 2048
EPS = 1e-6
LAM_INIT = 0.8 - 0.6 * math.exp(0.0)
SLOPES = [2.0 ** (-8.0 * (h + 1) / 4) for h in range(4)]
QSCALE = 128 ** -0.5
NEG = -30000.0
ENGS = ("sp", "act", "dve", "pool", "pe")


def _na_tile_idx(R, qrows, krows):
    out = -np.ones((128, 128), np.int64)
    for qi in range(128):
        r = qrows[qi // 64]
        c = qi % 64
        if r < 0 or r >= R:
            continue
        rs = min(max(r - 4, 0), R - 8)
        cs = min(max(c - 8, 0), 64 - 16)
        for ki in range(128):
            kr = krows[ki // 64]
            kc = ki % 64
            if kr < 0 or kr >= R:
                continue
            if rs <= kr < rs + 8 and cs <= kc < cs + 16:
                out[ki, qi] = (kr - r + 7) * 31 + (kc - c + 15)
    return out


NA_S_REL = {0: [0, 1, 2, 3], 1: [-1, 0, 1, 2], 14: [-2, -1, 0, 1], 15: [-3, -2, -1, 0]}
NA_P_REL = {0: [-2, -1, 0, 1, 2, 3], 15: [-3, -2, -1, 0, 1, 2]}


def na_rel(jobtype, j):
    d = NA_S_REL if jobtype == "s" else NA_P_REL
    return d.get(j, [-2, -1, 0, 1, 2])


def na_groups(jobtype):
    if jobtype == "s":
        keys = {0: "j0", 1: "j1", 14: "j14", 15: "j15"}
    else:
        keys = {0: "j0", 1: "j1", 14: "j14", 15: "j15"}
    order = ["j0", "j1", "int", "j14", "j15"]
    rep = {"j0": 0, "j1": 1, "int": 7, "j14": 14, "j15": 15}
    offs = {}
    o = 0
    for k in order:
        offs[k] = o
        o += len(na_rel(jobtype, rep[k]))
    return keys, order, rep, offs, o


def build_na_idx(jobtype, half):
    keys, order, rep, offs, ntile = na_groups(jobtype)
    tiles = []
    for k in order:
        j = rep[k]
        if jobtype == "s":
            R = 32
            jg = j
        else:
            R = 64
            jg = half * 16 + j
        for rel in na_rel(jobtype, j):
            kt = jg + rel
            tiles.append(_na_tile_idx(R, [2 * jg, 2 * jg + 1], [2 * kt, 2 * kt + 1]))
    return np.stack(tiles, 0)


def build_tables(half):
    kp = np.arange(128, dtype=np.float32)[:, None]
    qf = np.arange(512, dtype=np.float32)[None, :]
    dconst = np.zeros((128, 5, 512), np.float32)
    dconst[:, 0, :] = qf - kp
    for i in range(4):
        dconst[:, 1 + i, :] = np.abs(qf - kp - 128.0 * i)
    tabc = np.zeros((128, 4 * 33), np.float32)
    for h in range(4):
        for m in range(33):
            tabc[:, h * 33 + m] = -SLOPES[h] * m * 128.0
    tabo = np.zeros((128, 4 + 256), np.float32)
    for h in range(4):
        tabo[:, h] = SLOPES[h] if half == 0 else -SLOPES[h]
        for g in range(4):
            for kt in range(16):
                m = (16 + kt - 4 * g) if half == 0 else (16 + 4 * g - kt)
                tabo[:, 4 + h * 64 + g * 16 + kt] = -SLOPES[h] * m * 128.0
    slp = np.zeros((128, 8), np.float32)
    for h in range(4):
        slp[:, h] = SLOPES[h]
        slp[:, 4 + h] = -SLOPES[h]
    return dconst, tabc, tabo, slp


class Sem:
    def __init__(self, h):
        self.h = h
        self.n = 0


class Slot:
    __slots__ = ("ap", "wr", "rd")

    def __init__(self, ap=None):
        self.ap = ap
        self.wr = {}
        self.rd = {}


class Builder:
    def __init__(self, nc, es):
        self.nc = nc
        self.es = es
        self.q = {k: [] for k in ENGS}
        self.seen = {k: {} for k in ENGS}
        self.prog = {k: self.sem("prog_" + k) for k in ("act", "dve", "pool", "pe")}
        self.dsems = []
        self.nins = 0

    def sem(self, name):
        return Sem(self.es.enter_context(self.nc.semaphore(name)))

    def dsem(self, name):
        s = self.sem(name)
        self.dsems.append(s)
        return s

    def wait(self, eng, tok):
        if tok is None:
            return
        s, v = tok[0], tok[1]
        if self.seen[eng].get(id(s), 0) >= v:
            return
        self.seen[eng][id(s)] = v
        self.q[eng].append(lambda e, s=s, v=v: e.wait_ge(s.h, v))
        self.nins += 1

    def _deps(self, eng, reads, writes, extra, is_dma, nowaw=False):
        for s in reads:
            for t in list(s.wr.values()):
                self.wait(eng, t)
        for s in writes:
            for t in list(s.rd.values()):
                self.wait(eng, t)
            if not nowaw:
                for t in list(s.wr.values()):
                    self.wait(eng, t)
        for t in extra:
            self.wait(eng, t)

    def _record(self, tok, reads, writes, cont=()):
        k = id(tok[0])
        for s in reads:
            s.rd[k] = tok
        for s in writes:
            s.wr[k] = tok
        for s in cont:
            s.wr[k] = tok

    def op(self, eng, fn, reads=(), writes=(), extra=(), nowaw=False):
        self._deps(eng, reads, writes, extra, False, nowaw)
        self.nins += 1
        ps = self.prog[eng]
        ps.n += 1
        tok = (ps, ps.n, eng)
        self.q[eng].append(lambda e, fn=fn, ps=ps: fn(e).then_inc(ps.h, 1))
        self._record(tok, reads, writes)
        return tok

    def group(self, eng, fns, reads=(), writes=(), cont=(), extra=()):
        self._deps(eng, reads, writes, extra, False)
        for fn in fns[:-1]:
            self.q[eng].append(fn)
        self.nins += len(fns)
        ps = self.prog[eng]
        ps.n += 1
        tok = (ps, ps.n, eng)
        fn = fns[-1]
        self.q[eng].append(lambda e, fn=fn, ps=ps: fn(e).then_inc(ps.h, 1))
        self._record(tok, reads, writes, cont)
        return tok

    def dma(self, eng, out, in_, dsem, reads=(), writes=(), extra=()):
        self._deps(eng, reads, writes, extra, True)
        dsem.n += 16
        tok = (dsem, dsem.n, "dma")
        self.q[eng].append(lambda e, o=out, i=in_, d=dsem: e.dma_start(out=o, in_=i).then_inc(d.h, 16))
        self.nins += 1
        self._record(tok, reads, writes)
        return tok

    def batch_fix(self, dsem, slots):
        tok = (dsem, dsem.n, "dma")
        for s in slots:
            if id(dsem) in s.wr:
                s.wr[id(dsem)] = tok
            if id(dsem) in s.rd:
                s.rd[id(dsem)] = tok

    def barrier(self):
        toks = [(self.prog[k], self.prog[k].n, k) for k in self.prog]
        toks += [(d, d.n, "dma") for d in self.dsems]
        for eng in ENGS:
            for t in toks:
                if t[1] > 0:
                    self.wait(eng, t)

    def emit(self, block):
        q = self.q

        @block.sync
        def _(e):
            for fn in q["sp"]:
                fn(e)

        @block.scalar
        def _(e):
            for fn in q["act"]:
                fn(e)

        @block.vector
        def _(e):
            for fn in q["dve"]:
                fn(e)

        @block.gpsimd
        def _(e):
            for fn in q["pool"]:
                fn(e)

        @block.tensor
        def _(e):
            for fn in q["pe"]:
                fn(e)


class Arena:
    def __init__(self, ap, n):
        self.ap = ap
        self.n = n
        self.off = 0

    def alloc(self, shape, dt):
        cnt = int(np.prod(shape))
        nb = cnt * (2 if dt == F32 else 1)
        nb = (nb + 15) // 16 * 16
        assert self.off + nb <= self.n, ("SBUF arena overflow", self.off, nb, self.n)
        v = self.ap[:, self.off:self.off + (cnt * (2 if dt == F32 else 1))]
        self.off += nb
        if dt == F32:
            v = v.bitcast(F32)
        if len(shape) == 2:
            v = v.rearrange("p (a b) -> p a b", a=shape[0], b=shape[1])
        elif len(shape) == 3:
            v = v.rearrange("p (a b c) -> p a b c", a=shape[0], b=shape[1], c=shape[2])
        return v

    def slot(self, shape, dt):
        return Slot(self.alloc(shape, dt))

    def slot_top(self, shape, dt):
        cnt = int(np.prod(shape)) * (2 if dt == F32 else 1)
        nb = (cnt + 15) // 16 * 16
        self.n -= nb
        assert self.n >= self.off
        save_off, save_n = self.off, self.n
        self.off, self.n = self.n, self.n + nb
        v = self.alloc(shape, dt)
        self.off, self.n = save_off, save_n
        return Slot(v)


ARENA_ELEMS = 101 * 1024


def build_nc(stage=99, dbg=False):
    nc = bass.Bass("TRN2", target_bir_lowering=False)
    es = ExitStack()

    def din(name, shape, dt=F32):
        return nc.dram_tensor(name, list(shape), dt, kind="ExternalInput").ap()

    def dscr(name, shape, dt=BF16, out=False):
        kind = "ExternalOutput" if (out and dbg) else "Internal"
        return nc.dram_tensor(name, list(shape), dt, kind=kind).ap()

    xq = din("xq", [NJOB, T, D])
    xo = din("xo", [T, D])
    cT_d = din("cT", [128, 64])
    w_ada = din("w_ada", [D, 6 * D])
    b_adaT_d = din("b_adaT", [128, 96])
    g_mixT_d = din("g_mixT", [128, 16])
    g_mlpT_d = din("g_mlpT", [128, 16])
    g_final_d = din("g_final", [1, D])
    g_subln_d = din("g_subln", [1, 256])
    lam4_d = din("lam4", [4, 128])
    w_in = din("w_in", [D, 10240])
    w_pa = din("w_pa", [1024, D])
    w_pb = din("w_pb", [1024, D])
    w_out = din("w_out", [D, D])
    w1 = din("w1", [D, 4 * D])
    w2 = din("w2", [4 * D, D])
    nab_s_d = din("nab_s", [8, 128, 21 * 128])
    nab_p_d = din("nab_p", [8, 128, 27 * 128])
    dconst_d = din("dconst", [128, 5 * 512])
    tabc_d = din("tabc", [128, 132])
    tabo_d = din("tabo", [128, 260])
    slp_d = din("slp", [128, 8])
    y = nc.dram_tensor("y", [NJOB, T, D], F32, kind="ExternalOutput").ap()

    wb_in = dscr("wb_in", [20, 128, 8192])
    wb_pa = dscr("wb_pa", [4, 128, 4096])
    wb_pb = dscr("wb_pb", [4, 128, 4096])
    wb_out = dscr("wb_out", [4, 128, 8192])
    wb_1 = dscr("wb_1", [16, 128, 8192])
    wb_2 = dscr("wb_2", [16, 128, 8192])
    SK = [4096, 2048, 2048]
    SKN = [2560, 2048, 2048]
    qaT = [dscr(f"qaT{j}", [1024, T], out=True) for j in range(NJOB)]
    kaT = [dscr(f"kaT{j}", [1024, SK[j]], out=True) for j in range(NJOB)]
    va = [dscr(f"va{j}", [SK[j], 1024], out=True) for j in range(NJOB)]
    qnT = [dscr(f"qnT{j}", [1024, T], out=True) for j in range(NJOB)]
    knT = [dscr(f"knT{j}", [1024, SKN[j]], out=True) for j in range(NJOB)]
    vn = [dscr(f"vn{j}", [SKN[j], 1024], out=True) for j in range(NJOB)]
    gT = [dscr(f"gT{j}", [4096, T], out=True) for j in range(NJOB)]
    oa_s = [dscr(f"oa{j}", [T, 1024], out=True) for j in range(NJOB)]
    on_s = [dscr(f"on{j}", [T, 1024], out=True) for j in range(NJOB)]
    dbg_ada = nc.dram_tensor("dbg_ada", [128, 96 * 4], F32, kind="ExternalOutput").ap() if dbg else None

    arena_t = es.enter_context(nc.sbuf_tensor("arena", [128, ARENA_ELEMS], BF16))
    AR = Arena(arena_t[:], ARENA_ELEMS)
    banks = [Slot(es.enter_context(nc.psum_tensor(f"bank{i}", [128, 512], F32))[:]) for i in range(8)]
    B = Builder(nc, es)

    DS = [B.dsem(f"ds{i}") for i in range(24)]
    wsem_in = [B.sem(f"wsin{i}") for i in range(4)]
    wsem_pa = B.sem("wspa")
    wsem_pb = B.sem("wspb")
    wsem_out = B.sem("wsout")
    wsem_1 = B.sem("ws1")
    wsem_2 = B.sem("ws2")

    W_in = [Slot() for _ in range(4)]
    W_pa, W_pb, W_out, W_1, W_2 = Slot(), Slot(), Slot(), Slot(), Slot()
    S_qaT = [Slot() for _ in range(NJOB)]
    S_kaT = [Slot() for _ in range(NJOB)]
    S_va = [Slot() for _ in range(NJOB)]
    S_qnT = [Slot() for _ in range(NJOB)]
    S_knT = [Slot() for _ in range(NJOB)]
    S_vn = [Slot() for _ in range(NJOB)]
    S_gT = [Slot() for _ in range(NJOB)]
    S_oa = [Slot() for _ in range(NJOB)]
    S_on = [Slot() for _ in range(NJOB)]
    S_y = Slot()

    ident = AR.slot([128], BF16)
    identf = AR.slot([128], F32)
    onesf = AR.slot([128], F32)
    epsc = AR.slot([1], F32)
    adaT = AR.slot([96, 4], F32)
    s1p = AR.slot([16, 4], F32)
    s2p = AR.slot([16, 4], F32)
    g_mixT = AR.slot([16], F32)
    g_mlpT = AR.slot([16], F32)
    b_adaT = AR.slot([96], F32)
    gf_bc = AR.slot([D], F32)
    gsub_bc = AR.slot([256], F32)
    neglam = AR.slot([1], F32)
    gt1_bc = AR.slot([D], F32)
    gt2_bc = AR.slot([D], F32)
    lamt = AR.slot([8], F32)

    def mk_ident(slot):
        B.op("pool", lambda e, a=slot.ap: e.memset(a, 0.0), writes=[slot])
        B.op("pool", lambda e, a=slot.ap: e.affine_select(out=a, in_=a, compare_op=ALU.not_equal, fill=1.0,
                                                          base=0, pattern=[[-1, 128]], channel_multiplier=1),
             reads=[slot], writes=[slot])

    mk_ident(ident)
    mk_ident(identf)
    B.op("pool", lambda e: e.memset(onesf.ap, 1.0), writes=[onesf])
    B.op("pool", lambda e: e.memset(epsc.ap, EPS), writes=[epsc])
    B.op("pool", lambda e: e.memset(lamt.ap, 0.0), writes=[lamt])

    def conv(dst, src, k, sem, slot):
        B.dma("pool", dst.rearrange("p (k n) -> p k n", k=k, n=512), src.rearrange("(k p) n -> p k n", p=128), sem, writes=[slot])

    for nb in (2, 3, 4, 0, 1, 5, 6, 7, 8, 9, 10, 11, 12, 13, 14, 15, 16, 17, 18, 19):
        conv(wb_in[nb], w_in[:, nb * 512:(nb + 1) * 512], 16, wsem_in[nb // 5], W_in[nb // 5])
    def conv_rest(gate, part):
        B.wait("pool", gate)
        if part == 0:
            for nb in range(4):
                conv(wb_pa[nb], w_pa[:, nb * 512:(nb + 1) * 512], 8, wsem_pa, W_pa)
                conv(wb_pb[nb], w_pb[:, nb * 512:(nb + 1) * 512], 8, wsem_pb, W_pb)
            for nb in range(4):
                conv(wb_out[nb], w_out[:, nb * 512:(nb + 1) * 512], 16, wsem_out, W_out)
        elif part == 1:
            for nb in range(16):
                conv(wb_1[nb], w1[:, nb * 512:(nb + 1) * 512], 16, wsem_1, W_1)
        else:
            for qq in range(4):
                for cg in range(4):
                    conv(wb_2[qq * 4 + cg], w2[qq * 2048:(qq + 1) * 2048, cg * 512:(cg + 1) * 512], 16, wsem_2, W_2)

    B.dma("sp", g_mixT.ap, g_mixT_d, DS[0], writes=[g_mixT])
    B.dma("sp", g_mlpT.ap, g_mlpT_d, DS[0], writes=[g_mlpT])
    B.dma("sp", b_adaT.ap, b_adaT_d, DS[0], writes=[b_adaT])
    B.dma("sp", gf_bc.ap, g_final_d[0, :].partition_broadcast(128), DS[0], writes=[gf_bc])
    B.dma("sp", gsub_bc.ap, g_subln_d[0, :].partition_broadcast(128), DS[0], writes=[gsub_bc])
    B.batch_fix(DS[0], [g_mixT, g_mlpT, b_adaT, gf_bc, gsub_bc])

    m0 = AR.off
    lamv = AR.slot([4, 128], F32)
    ljunk = AR.slot([128], F32)
    for i in range(4):
        B.dma("sp", lamv.ap[:, i, :], lam4_d[i, :].partition_broadcast(128), DS[1], writes=[lamv])
    B.op("dve", lambda e: e.scalar_tensor_tensor(out=ljunk.ap, in0=lamv.ap[:, 0, :], scalar=1.0, in1=lamv.ap[:, 1, :],
                                                 op0=ALU.mult, op1=ALU.mult, accum_out=lamt.ap[:, 0:1]),
         reads=[lamv, lamt], writes=[ljunk, lamt])
    B.op("dve", lambda e: e.scalar_tensor_tensor(out=ljunk.ap, in0=lamv.ap[:, 2, :], scalar=1.0, in1=lamv.ap[:, 3, :],
                                                 op0=ALU.mult, op1=ALU.mult, accum_out=lamt.ap[:, 1:2]),
         reads=[lamv, lamt], writes=[ljunk, lamt])
    B.op("act", lambda e: e.activation(out=lamt.ap[:, 2:4], in_=lamt.ap[:, 0:2], func=AF.Exp), reads=[lamt], writes=[lamt])
    B.op("dve", lambda e: e.tensor_tensor(out=lamt.ap[:, 4:5], in0=lamt.ap[:, 3:4], in1=lamt.ap[:, 2:3], op=ALU.subtract),
         reads=[lamt], writes=[lamt])
    B.op("dve", lambda e: e.tensor_scalar(out=neglam.ap, in0=lamt.ap[:, 4:5], scalar1=-LAM_INIT, scalar2=None, op0=ALU.add),
         reads=[lamt], writes=[neglam])
    B.op("dve", lambda e: e.tensor_scalar(out=gsub_bc.ap, in0=gsub_bc.ap, scalar1=1.0 - LAM_INIT, scalar2=None, op0=ALU.mult),
         reads=[gsub_bc], writes=[gsub_bc])

    cT = AR.slot_top([16, 4], F32)
    wad = [AR.slot_top([16, 256], F32) for _ in range(2)]
    tmp16 = AR.slot_top([16, 4], F32)
    base_off2 = AR.off
    B.dma("sp", cT.ap.rearrange("p a b -> p (a b)"), cT_d, DS[16], writes=[cT])
    ada_state = {"i": 0, "issued": 0}
    ADA_ORDER = list(range(48))

    def ada_issue():
        i = ada_state["issued"]
        if i >= 48:
            return
        ht = ADA_ORDER[i]
        wt = wad[i % 2]
        B.dma("sp", wt.ap, w_ada[:, ht * 256:(ht + 1) * 256].rearrange("(k p) n -> p k n", p=128), DS[2 + i % 2], writes=[wt])
        ada_state["issued"] += 1

    def ada_half_tiles(n, bank_fn):
        for _ in range(n):
            i = ada_state["i"]
            if i >= 48:
                return
            if ada_state["issued"] <= i:
                ada_issue()
            ht = ADA_ORDER[i]
            wt = wad[i % 2]
            bk = bank_fn()
            fns = []
            for m_ in range(2):
                for kc in range(16):
                    fns.append(lambda e, bk=bk, wt=wt, m_=m_, kc=kc: e.matmul(
                        bk.ap[:, m_ * 4:m_ * 4 + 4], wt.ap[:, kc, m_ * 128:(m_ + 1) * 128], cT.ap[:, kc, :],
                        start=(kc == 0), stop=(kc == 15)))
            B.group("pe", fns, reads=[wt, cT], writes=[bk])
            ada_state["i"] += 1
            ada_issue()
            B.op("dve", lambda e, bk=bk, ht=ht: e.tensor_copy(
                out=adaT.ap[:, ht * 2:(ht + 1) * 2, :], in_=bk.ap[:, 0:8].rearrange("p (a b) -> p a b", a=2, b=4)),
                reads=[bk], writes=[adaT], nowaw=True)

    def ada_finish(lo, hi, parts):
        for s_ in range(4):
            B.op("dve", lambda e, s_=s_: e.tensor_tensor(out=adaT.ap[:, lo:hi, s_], in0=adaT.ap[:, lo:hi, s_],
                                                         in1=b_adaT.ap[:, lo:hi], op=ALU.add),
                 reads=[adaT, b_adaT], writes=[adaT])
        for (dst, lo_, g) in parts:
            B.op("dve", lambda e, lo_=lo_: e.tensor_scalar(out=tmp16.ap, in0=adaT.ap[:, lo_:lo_ + 16, :], scalar1=1.0, scalar2=None,
                                                           op0=ALU.add), reads=[adaT], writes=[tmp16])
            for s_ in range(4):
                B.op("dve", lambda e, dst=dst, g=g, s_=s_: e.tensor_tensor(out=dst.ap[:, :, s_], in0=tmp16.ap[:, :, s_], in1=g.ap,
                                                                           op=ALU.mult), reads=[tmp16, g], writes=[dst])

    _pb = [0]

    def _p0bank():
        _pb[0] += 1
        return banks[_pb[0] % 2]

    ada_issue()
    ada_half_tiles(16, _p0bank)
    ada_finish(0, 32, [(s1p, 16, g_mixT)])
    B.barrier()
    AR.off = base_off2

    def make_gt_bc(js):
        m = AR.off
        diag = [AR.slot([128], F32) for _ in range(2)]
        i = 0
        for (dst, lo) in ((gt1_bc, 32), (gt2_bc, 80)):
            for kc4 in range(4):
                bk = banks[i % 2]
                for kk in range(4):
                    kc = kc4 * 4 + kk
                    dg = diag[kc % 2]
                    B.op("dve", lambda e, dg=dg, lo=lo, kc=kc: e.tensor_scalar(
                        out=dg.ap, in0=identf.ap, scalar1=adaT.ap[:, lo + kc, js:js + 1], scalar2=None, op0=ALU.mult),
                        reads=[identf, adaT], writes=[dg])
                    B.group("pe", [lambda e, bk=bk, dg=dg, kk=kk: e.matmul(
                        bk.ap[:, kk * 128:(kk + 1) * 128], onesf.ap, dg.ap, start=True, stop=True)],
                        reads=[onesf, dg], writes=[bk] if kk == 0 else [], cont=[bk] if kk > 0 else [])
                B.op("act", lambda e, bk=bk, dst=dst, kc4=kc4: e.activation(
                    out=dst.ap[:, kc4 * 512:(kc4 + 1) * 512], in_=bk.ap, func=AF.Copy), reads=[bk], writes=[dst])
                i += 1
        B.barrier()
        AR.off = m

    def phase_A(xsrc, js, hT):
        m = AR.off
        xt = [AR.slot([D], F32) for _ in range(2)]
        xn = [AR.slot([D], BF16) for _ in range(2)]
        junk = AR.slot([D], BF16)
        st = AR.slot([16, 4], F32)
        KA = int(os.environ.get("KA", "9"))
        B.op("pool", lambda e: e.memset(st.ap, 0.0), writes=[st])
        for tt in range(16):
            if KA <= 0:
                continue
            x_ = xt[tt % 2]
            B.dma("sp", x_.ap, xsrc[tt * 128:(tt + 1) * 128, :], DS[tt % 2], writes=[x_])
            B.op("act", lambda e, x_=x_, tt=tt: e.activation(out=junk.ap, in_=x_.ap, func=AF.Square,
                                                             accum_out=st.ap[:, tt, 0:1]),
                 reads=[x_, st], writes=[junk, st])
            B.op("act", lambda e, tt=tt: e.activation(out=st.ap[:, tt, 1:2], in_=st.ap[:, tt, 0:1], func=AF.Ln,
                                                      bias=epsc.ap, scale=1.0 / D), reads=[st, epsc], writes=[st])
            B.op("act", lambda e, tt=tt: e.activation(out=st.ap[:, tt, 2:3], in_=st.ap[:, tt, 1:2], func=AF.Exp, scale=-0.5),
                 reads=[st], writes=[st])
            if KA <= 1:
                continue
            xn_ = xn[tt % 2]
            B.op("act", lambda e, x_=x_, xn_=xn_, tt=tt: e.activation(
                out=xn_.ap, in_=x_.ap, func=AF.Copy, scale=st.ap[:, tt, 2:3]),
                reads=[x_, st], writes=[xn_])
            if KA <= 2:
                continue
            for k4 in range(4):
                bk = banks[(tt * 4 + k4) % 8]
                pv = bk.ap.bitcast(BF16)
                fns = [lambda e, pv=pv, xn_=xn_, kk=kk, k4=k4: e.transpose(
                    pv[:, kk * 128:(kk + 1) * 128], xn_.ap[:, (k4 * 4 + kk) * 128:(k4 * 4 + kk + 1) * 128], ident.ap)
                    for kk in range(4)]
                B.group("pe", fns, reads=[xn_, ident], writes=[bk])
                if KA <= 3:
                    continue
                for kk in range(4):
                    kc = k4 * 4 + kk
                    dst = hT.ap[:, kc, tt * 128:(tt + 1) * 128]
                    if KA == 6:
                        B.op("act", lambda e, pv=pv, kk=kk, kc=kc, dst=dst: e.activation(
                            out=dst, in_=pv[:, kk * 128:(kk + 1) * 128], func=AF.Identity,
                            bias=adaT.ap[:, kc, js:js + 1], scale=s1p.ap[:, kc, js:js + 1]),
                            reads=[bk, adaT, s1p], writes=[hT], nowaw=True)
                    else:
                        B.op("dve", lambda e, pv=pv, kk=kk, kc=kc, dst=dst: e.tensor_scalar(
                            out=dst, in0=pv[:, kk * 128:(kk + 1) * 128], scalar1=s1p.ap[:, kc, js:js + 1],
                            scalar2=adaT.ap[:, kc, js:js + 1], op0=ALU.mult, op1=ALU.add),
                            reads=[bk, adaT, s1p], writes=[hT], nowaw=True)
        B.barrier()
        AR.off = m

    ps = {"i": 0, "bi": 0, "si": 0}

    class WStream:
        def __init__(self, bufs, sems, specs):
            self.bufs, self.sems, self.specs = bufs, sems, specs
            self.i = 0
            self.u = 0

        def get(self):
            nbuf = len(self.bufs)
            while self.i < len(self.specs) and self.i < self.u + nbuf:
                parts, rds = self.specs[self.i]
                wt = self.bufs[self.i % nbuf]
                sem = self.sems[self.i % nbuf]
                for (k0, k1, src) in parts:
                    B.dma("sp", wt.ap[:, k0:k1, :].rearrange("p a b -> p (a b)"), src, sem, reads=rds, writes=[wt])
                B.batch_fix(sem, [wt])
                self.i += 1
            wt = self.bufs[self.u % nbuf]
            self.u += 1
            return wt

    wstream = [None]

    def load_w(wtiles, nb):
        return wstream[0].get()

    def evac(kind, o, bk, sg, par):
        if kind == "sig":
            B.op("act", lambda e: e.activation(out=o, in_=bk.ap, func=AF.Sigmoid), reads=[bk], writes=[sg], nowaw=True)
        elif kind == "q":
            if par == 0:
                B.op("act", lambda e: e.activation(out=o, in_=bk.ap, func=AF.Copy, scale=QSCALE), reads=[bk], writes=[sg], nowaw=True)
            else:
                B.op("dve", lambda e: e.tensor_scalar(out=o, in0=bk.ap, scalar1=QSCALE, scalar2=None, op0=ALU.mult),
                     reads=[bk], writes=[sg], nowaw=True)
        else:
            if par == 0:
                B.op("act", lambda e: e.activation(out=o, in_=bk.ap, func=AF.Copy), reads=[bk], writes=[sg], nowaw=True)
            else:
                B.op("dve", lambda e: e.tensor_copy(out=o, in_=bk.ap), reads=[bk], writes=[sg], nowaw=True)

    def mm_T(bk, wt, m_, hT, t0, t1, c0):
        return [lambda e, kc=kc: e.matmul(bk.ap[:, c0:c0 + (t1 - t0)], wt.ap[:, kc, m_ * 128:(m_ + 1) * 128], hT.ap[:, kc, t0:t1],
                                          start=(kc == 0), stop=(kc == 15)) for kc in range(16)]

    def proj_T(hT, wt, m_, dsts, kind, halo=False):
        sg = stg[ps["si"] % 2]
        dsm = DS[6 + ps["si"] % 2]
        ps["si"] += 1
        if not halo:
            bks = [banks[(ps["bi"] + tg) % 8] for tg in range(4)]
            ps["bi"] += 4
            fns = []
            for kc in range(16):
                for tg in range(4):
                    fns.append(lambda e, bk=bks[tg], kc=kc, tg=tg: e.matmul(
                        bk.ap, wt.ap[:, kc, m_ * 128:(m_ + 1) * 128], hT.ap[:, kc, tg * 512:(tg + 1) * 512],
                        start=(kc == 0), stop=(kc == 15)))
            B.group("pe", fns, reads=[wt, hT], writes=bks)
            for tg in range(4):
                evac(kind, sg.ap[:, tg, :], bks[tg], sg, tg % 2)
        else:
            bk = banks[ps["bi"] % 8]
            ps["bi"] += 1
            fns = mm_T(bk, wt, m_, hT, T - 256, T, 0) + mm_T(bk, wt, m_, hT, 0, 256, 256)
            B.group("pe", fns, reads=[wt, hT], writes=[bk])
            evac(kind, sg.ap[:, 0, :], bk, sg, 0)
        for (dst, c0, c1, dslot) in dsts:
            B.dma("sp", dst, sg.ap.rearrange("p a b -> p (a b)")[:, c0:c1], dsm, reads=[sg], writes=[dslot])

    def proj_V(hT, wt, c0, dst, dslot, tts, d0):
        sg = stg[ps["si"] % 2]
        dsm = DS[6 + ps["si"] % 2]
        ps["si"] += 1
        for ii, tt in enumerate(tts):
            bk = banks[ps["bi"] % 8]
            ps["bi"] += 1
            fns = [lambda e, bk=bk, kc=kc, tt=tt: e.matmul(
                bk.ap, hT.ap[:, kc, tt * 128:(tt + 1) * 128], wt.ap[:, kc, :], start=(kc == 0), stop=(kc == 15))
                for kc in range(16)]
            B.group("pe", fns, reads=[wt, hT], writes=[bk])
            evac("k", sg.ap[:, ii, :], bk, sg, ii % 2)
        n = len(tts)
        B.dma("sp", dst[d0 * 128:(d0 + n) * 128, c0:c0 + 512].rearrange("(a p) n -> p a n", p=128),
              sg.ap[:, 0:n, :], dsm, reads=[sg], writes=[dslot])

    stg = []

    def phase_proj(j, xsrc, js, full=True):
        m = AR.off
        hT = AR.slot([16, T], BF16)
        phase_A(xsrc, js, hT)
        if os.environ.get("KSTOP") == "A":
            return
        wtiles = [AR.slot([16, 512], BF16) for _ in range(3)]
        stg.clear()
        stg.extend([AR.slot([4, 512], BF16) for _ in range(2)])
        order = list(range(20)) if full else [2, 3, 4, 5, 8, 9, 10, 11]
        wstream[0] = WStream(wtiles, [DS[4], DS[5], DS[8]], [([(0, 16, wb_in[nb])], [W_in[nb // 5]]) for nb in order])
        if full:
            knoff = 256 if j == 0 else 0
            for nb in range(20):
                if j == 0 and nb == 2:
                    conv_rest((B.prog["pe"], B.prog["pe"].n, "pe"), 1)
                if j == 0 and nb == 9:
                    conv_rest((B.prog["pe"], B.prog["pe"].n, "pe"), 2)
                wt = load_w(wtiles, nb)
                if nb in (4, 5):
                    for g4 in range(4):
                        proj_V(hT, wt, (nb % 2) * 512, va[j], S_va[j], [g4 * 4 + i for i in range(4)], g4 * 4)
                elif nb in (10, 11):
                    for g4 in range(4):
                        proj_V(hT, wt, (nb % 2) * 512, vn[j], S_vn[j], [g4 * 4 + i for i in range(4)], g4 * 4 + knoff // 128)
                else:
                    for m_ in range(4):
                        r0 = (nb % 2) * 512 + m_ * 128
                        if nb in (0, 1):
                            d, sl, kind = qaT[j][r0:r0 + 128, :], S_qaT[j], "q"
                        elif nb in (2, 3):
                            d, sl, kind = kaT[j][r0:r0 + 128, 0:T], S_kaT[j], "k"
                        elif nb in (6, 7):
                            d, sl, kind = qnT[j][r0:r0 + 128, :], S_qnT[j], "q"
                        elif nb in (8, 9):
                            d, sl, kind = knT[j][r0:r0 + 128, knoff:knoff + T], S_knT[j], "k"
                        else:
                            r0g = (nb - 12) * 512 + m_ * 128
                            d, sl, kind = gT[j][r0g:r0g + 128, :], S_gT[j], "sig"
                        proj_T(hT, wt, m_, [(d, 0, 2048, sl)], kind)
        else:
            def pbank():
                b_ = banks[ps["bi"] % 8]
                ps["bi"] += 1
                return b_
            for nb in (2, 3):
                wt = load_w(wtiles, nb)
                for m_ in range(4):
                    r0 = (nb % 2) * 512 + m_ * 128
                    proj_T(hT, wt, m_, [(kaT[0][r0:r0 + 128, T:2 * T], 0, 2048, S_kaT[0])], "k")
                    ada_half_tiles(2, pbank)
            conv_rest((B.prog["pe"], B.prog["pe"].n, "pe"), 0)
            for nb in (4, 5):
                wt = load_w(wtiles, nb)
                for g4 in range(4):
                    proj_V(hT, wt, (nb % 2) * 512, va[0], S_va[0], [g4 * 4 + i for i in range(4)], 16 + g4 * 4)
                    ada_half_tiles(2, pbank)
            ada_half_tiles(48, pbank)
            ada_finish(32, 96, [(s2p, 64, g_mlpT)])
            for nb in (8, 9):
                wt = load_w(wtiles, nb)
                for m_ in range(4):
                    r0 = (nb % 2) * 512 + m_ * 128
                    proj_T(hT, wt, m_, [(knT[0][r0:r0 + 128, 0:256], 0, 256, S_knT[0]),
                                        (knT[0][r0:r0 + 128, 2304:2560], 256, 512, S_knT[0])], "k", halo=True)
            for nb in (10, 11):
                wt = load_w(wtiles, nb)
                proj_V(hT, wt, (nb % 2) * 512, vn[0], S_vn[0], [14, 15], 0)
                proj_V(hT, wt, (nb % 2) * 512, vn[0], S_vn[0], [0, 1], 18)
        B.barrier()
        AR.off = m

    def phase_DA(j):
        m = AR.off
        Sk = SK[j]
        nkt = Sk // 128
        dcon = AR.slot([5, 512], F32)
        tabc = AR.slot([132], F32)
        tabo = AR.slot([260], F32)
        slp = AR.slot([8], F32)
        B.dma("sp", dcon.ap.rearrange("p a b -> p (a b)"), dconst_d, DS[8], writes=[dcon])
        B.dma("sp", tabc.ap, tabc_d, DS[8], writes=[tabc])
        B.dma("sp", tabo.ap, tabo_d, DS[8], writes=[tabo])
        B.dma("sp", slp.ap, slp_d, DS[8], writes=[slp])
        B.batch_fix(DS[8], [dcon, tabc, tabo, slp])
        qT_h = [AR.slot([2, T], BF16) for _ in range(2)]
        kT_h = [AR.slot([2, Sk], BF16) for _ in range(2)]
        V_h = [AR.slot([nkt, 257], BF16) for _ in range(2)]
        for v_ in V_h:
            B.op("dve", lambda e, v_=v_: e.memset(v_.ap[:, :, 256:257], 1.0), writes=[v_])
        NBUF = 4
        LOOK = 3
        Sb = [AR.slot([512], F32) for _ in range(NBUF)]
        PT = [AR.slot([512], BF16) for _ in range(NBUF)]
        Oc = [[AR.slot([257], F32) for _ in range(4)] for _ in range(2)]
        tmpq = [AR.slot([256], F32) for _ in range(4)]
        oaf = [AR.slot([256], F32) for _ in range(4)]
        stq = [AR.slot([8], F32) for _ in range(4)]
        junk = AR.slot([256], BF16)
        ostg = [AR.slot([4, 256], BF16) for _ in range(2)]
        Ob = banks[0:4]
        Sbk = banks[4:8]
        cnt = [0]

        def emit_S(q_, k_, c, g, kt):
            bk = Sbk[cnt[0] % 4]
            B.group("pe", [lambda e: e.matmul(bk.ap, k_.ap[:, c, kt * 128:(kt + 1) * 128], q_.ap[:, c, g * 512:(g + 1) * 512],
                                              start=True, stop=True)], reads=[k_, q_], writes=[bk])
            r = (cnt[0], bk)
            cnt[0] += 1
            return r

        def emit_rest(v_, h, g, kt, step_, bk):
            if j == 0 and kt >= 16:
                a_ap = tabo.ap[:, h:h + 1]
                ci = 4 + h * 64 + g * 16 + (kt - 16)
                b_ap = tabo.ap[:, ci:ci + 1]
                dsel = 0
                rd = [tabo]
            else:
                delta = g * 512 - kt * 128
                if delta >= 128:
                    a_ap = slp.ap[:, 4 + h:5 + h]
                    mi = delta // 128
                    dsel = 0
                elif delta <= -512:
                    a_ap = slp.ap[:, h:h + 1]
                    mi = (-delta) // 128
                    dsel = 0
                else:
                    a_ap = slp.ap[:, 4 + h:5 + h]
                    mi = 0
                    dsel = 1 + (-delta) // 128
                b_ap = tabc.ap[:, h * 33 + mi:h * 33 + mi + 1]
                rd = [tabc, slp]
            sb = Sb[step_ % NBUF]
            pt = PT[step_ % NBUF]
            B.op("dve", lambda e: e.scalar_tensor_tensor(out=sb.ap, in0=dcon.ap[:, dsel, :], scalar=a_ap, in1=bk.ap,
                                                         op0=ALU.mult, op1=ALU.add), reads=[bk, dcon] + rd, writes=[sb])
            B.op("act", lambda e: e.activation(out=pt.ap, in_=sb.ap, func=AF.Exp, bias=b_ap, scale=1.0),
                 reads=[sb] + rd, writes=[pt])
            fns = [lambda e, qs=qs: e.matmul(Ob[qs].ap[:, 0:257], pt.ap[:, qs * 128:(qs + 1) * 128], v_.ap[:, kt, :],
                                             start=(kt == 0), stop=(kt == nkt - 1)) for qs in range(4)]
            B.group("pe", fns, reads=[pt, v_], writes=Ob if kt == 0 else [], cont=Ob if kt > 0 else [])

        deferred = []

        def flush(n=None):
            k = len(deferred) if n is None else min(n, len(deferred))
            for _ in range(k):
                deferred.pop(0)()

        def evac_O(h, g, c, og):
            if c == 0:
                flush()
            for qs in range(4):
                ob = Ob[qs]
                oc = Oc[c][qs]
                if qs % 2 == 0:
                    B.op("act", lambda e, ob=ob, oc=oc: e.activation(out=oc.ap, in_=ob.ap[:, 0:257], func=AF.Copy),
                         reads=[ob], writes=[oc])
                else:
                    B.op("dve", lambda e, ob=ob, oc=oc: e.tensor_copy(out=oc.ap, in_=ob.ap[:, 0:257]), reads=[ob], writes=[oc])
            if c == 0:
                return
            flush()
            stages = [[] for _ in range(10)]
            for qs in range(4):
                o0, o1, sq_, tq, of = Oc[0][qs], Oc[1][qs], stq[qs], tmpq[qs], oaf[qs]
                stages[0].append(lambda sq_=sq_: B.op("pool", lambda e: e.memset(sq_.ap, 0.0), writes=[sq_]))
                stages[1].append(lambda o0=o0, sq_=sq_: B.op("dve", lambda e: e.reciprocal(out=sq_.ap[:, 0:1], in_=o0.ap[:, 256:257]),
                                                             reads=[o0], writes=[sq_]))
                stages[2].append(lambda o1=o1, sq_=sq_: B.op("dve", lambda e: e.reciprocal(out=sq_.ap[:, 1:2], in_=o1.ap[:, 256:257]),
                                                             reads=[o1], writes=[sq_]))
                stages[3].append(lambda sq_=sq_: B.op("dve", lambda e: e.tensor_tensor(out=sq_.ap[:, 2:3], in0=sq_.ap[:, 1:2], in1=neglam.ap,
                                                                                       op=ALU.mult), reads=[sq_, neglam], writes=[sq_]))
                stages[4].append(lambda o0=o0, tq=tq, sq_=sq_: B.op("act", lambda e: e.activation(
                    out=tq.ap, in_=o0.ap[:, 0:256], func=AF.Copy, scale=sq_.ap[:, 0:1]), reads=[o0, sq_], writes=[tq]))
                stages[5].append(lambda o1=o1, tq=tq, of=of, sq_=sq_: B.op("dve", lambda e: e.scalar_tensor_tensor(
                    out=of.ap, in0=o1.ap[:, 0:256], scalar=sq_.ap[:, 2:3], in1=tq.ap, op0=ALU.mult, op1=ALU.add),
                    reads=[o1, tq, sq_], writes=[of]))
                stages[6].append(lambda of=of, sq_=sq_: B.op("act", lambda e: e.activation(
                    out=junk.ap, in_=of.ap, func=AF.Square, accum_out=sq_.ap[:, 3:4]), reads=[of, sq_], writes=[junk, sq_]))
                stages[7].append(lambda sq_=sq_: B.op("act", lambda e: e.activation(
                    out=sq_.ap[:, 4:5], in_=sq_.ap[:, 3:4], func=AF.Ln, bias=epsc.ap, scale=1.0 / 256), reads=[sq_, epsc], writes=[sq_]))
                stages[8].append(lambda sq_=sq_: B.op("act", lambda e: e.activation(
                    out=sq_.ap[:, 5:6], in_=sq_.ap[:, 4:5], func=AF.Exp, scale=-0.5), reads=[sq_], writes=[sq_]))
                stages[9].append(lambda qs=qs, of=of, sq_=sq_: B.op("dve", lambda e: e.scalar_tensor_tensor(
                    out=og.ap[:, qs, :], in0=of.ap, scalar=sq_.ap[:, 5:6], in1=gsub_bc.ap, op0=ALU.mult, op1=ALU.mult),
                    reads=[of, sq_, gsub_bc], writes=[og], nowaw=True))
            for st_ in stages:
                deferred.extend(st_)
            deferred.append(lambda: B.dma(
                "sp", oa_s[j][g * 512:(g + 1) * 512, h * 256:(h + 1) * 256].rearrange("(a p) n -> p a n", p=128),
                og.ap, DS[11 + (h * 4 + g) % 2], reads=[og], writes=[S_oa[j]]))

        def load_head(h):
            q_ = qT_h[h % 2]
            k_ = kT_h[h % 2]
            v_ = V_h[h % 2]
            dsm = DS[9 + h % 2]
            B.dma("sp", q_.ap, qaT[j][h * 256:(h + 1) * 256, :].rearrange("(c p) t -> p c t", p=128), dsm,
                  reads=[S_qaT[j]], writes=[q_])
            B.dma("sp", k_.ap, kaT[j][h * 256:(h + 1) * 256, :].rearrange("(c p) t -> p c t", p=128), dsm,
                  reads=[S_kaT[j]], writes=[k_])
            B.dma("sp", v_.ap[:, :, 0:256], va[j][:, h * 256:(h + 1) * 256].rearrange("(a p) n -> p a n", p=128),
                  dsm, reads=[S_va[j]], writes=[v_])
            B.batch_fix(dsm, [q_, k_, v_])

        load_head(0)
        load_head(1)
        pend = []

        def retire():
            a = pend.pop(0)
            emit_rest(*a[:6])
            flush(2)
            if a[3] == nkt - 1:
                evac_O(a[1], a[2], a[6], ostg[(a[1] * 4 + a[2]) % 2])
                if a[2] == 3 and a[6] == 1 and a[1] + 2 < 4:
                    load_head(a[1] + 2)

        for h in range(4):
            q_ = qT_h[h % 2]
            k_ = kT_h[h % 2]
            v_ = V_h[h % 2]
            for g in range(4):
                for c in range(2):
                    for kt in range(nkt):
                        st_, bk = emit_S(q_, k_, c, g, kt)
                        pend.append((v_, h, g, kt, st_, bk, c))
                        if len(pend) > LOOK:
                            retire()
        while pend:
            retire()
        flush()
        B.barrier()
        AR.off = m

    def phase_NA(j):
        m = AR.off
        jt = "p" if j == 0 else "s"
        keys, order, rep, offs, ntile = na_groups(jt)
        nab_d = nab_p_d if j == 0 else nab_s_d
        Skn = SKN[j]
        nkt = Skn // 128
        pad = 2 if j == 0 else 0
        q_h = [AR.slot([T], BF16) for _ in range(2)]
        k_h = [AR.slot([Skn], BF16) for _ in range(2)]
        v_h = [AR.slot([nkt, 129], BF16) for _ in range(2)]
        nb_h = [AR.slot([ntile, 128], F32) for _ in range(2)]
        for v_ in v_h:
            B.op("pool", lambda e, v_=v_: e.memset(v_.ap[:, :, 128:129], 1.0), writes=[v_])
        Sb = [AR.slot([6, 128], F32) for _ in range(3)]
        PT = [AR.slot([6, 128], BF16) for _ in range(3)]
        rl = AR.slot([16, 2], F32)
        ostg = [AR.slot([16, 128], BF16) for _ in range(2)]
        it = [0]

        def emit_S(q_, k_, jq):
            it_ = it[0]
            it[0] += 1
            rel = na_rel(jt, jq)
            n = len(rel)
            bA = banks[(it_ % 3) * 2]
            bB = banks[(it_ % 3) * 2 + 1]
            fns = []
            for i, r in enumerate(rel):
                kt = jq + r + pad
                tgt = bA.ap[:, i * 128:(i + 1) * 128] if i < 4 else bB.ap[:, (i - 4) * 128:(i - 3) * 128]
                fns.append(lambda e, tgt=tgt, kt=kt: e.matmul(
                    tgt, k_.ap[:, kt * 128:(kt + 1) * 128], q_.ap[:, jq * 128:(jq + 1) * 128], start=True, stop=True))
            B.group("pe", fns, reads=[k_, q_], writes=[bA, bB] if n > 4 else [bA])
            return (jq, it_, bA, bB, rel)

        def emit_rest(v_, nb_, og, jq, it_, bA, bB, rel):
            n = len(rel)
            off = offs[keys.get(jq, "int")]
            sb = Sb[it_ % 3]
            pt = PT[it_ % 3]
            na_ = min(n, 4)
            B.op("dve", lambda e: e.tensor_tensor(
                out=sb.ap[:, 0:na_, :], in0=bA.ap[:, 0:na_ * 128].rearrange("p (a b) -> p a b", a=na_, b=128),
                in1=nb_.ap[:, off:off + na_, :], op=ALU.add), reads=[bA, nb_], writes=[sb])
            if n > 4:
                nbb = n - 4
                B.op("dve", lambda e: e.tensor_tensor(
                    out=sb.ap[:, 4:4 + nbb, :], in0=bB.ap[:, 0:nbb * 128].rearrange("p (a b) -> p a b", a=nbb, b=128),
                    in1=nb_.ap[:, off + 4:off + 4 + nbb, :], op=ALU.add), reads=[bB, nb_], writes=[sb])
            B.op("act", lambda e: e.activation(out=pt.ap[:, 0:n, :], in_=sb.ap[:, 0:n, :], func=AF.Exp),
                 reads=[sb], writes=[pt])
            ob = banks[6 + it_ % 2]
            fns = []
            for i, r in enumerate(rel):
                kt = jq + r + pad
                fns.append(lambda e, i=i, kt=kt: e.matmul(
                    ob.ap[:, 0:129], pt.ap[:, i, :], v_.ap[:, kt, :], start=(i == 0), stop=(i == n - 1)))
            B.group("pe", fns, reads=[pt, v_], writes=[ob])

            def epilogue():
                B.op("dve", lambda e: e.reciprocal(out=rl.ap[:, jq, 0:1], in_=ob.ap[:, 128:129]), reads=[ob], writes=[rl])
                B.op("act", lambda e: e.activation(out=og.ap[:, jq, :], in_=ob.ap[:, 0:128], func=AF.Copy, scale=rl.ap[:, jq, 0:1]),
                     reads=[ob, rl], writes=[og])
            return epilogue

        def load_head(h):
            q_ = q_h[h % 2]
            k_ = k_h[h % 2]
            v_ = v_h[h % 2]
            nb_ = nb_h[h % 2]
            ds_ = DS[13 + h % 2]
            B.dma("sp", q_.ap, qnT[j][h * 128:(h + 1) * 128, :], ds_, reads=[S_qnT[j]], writes=[q_])
            B.dma("sp", k_.ap, knT[j][h * 128:(h + 1) * 128, :], ds_, reads=[S_knT[j]], writes=[k_])
            B.dma("sp", nb_.ap.rearrange("p a b -> p (a b)"), nab_d[h], ds_, writes=[nb_])
            B.dma("sp", v_.ap[:, :, 0:128], vn[j][:, h * 128:(h + 1) * 128].rearrange("(a p) n -> p a n", p=128),
                  ds_, reads=[S_vn[j]], writes=[v_])
            B.batch_fix(ds_, [q_, k_, nb_, v_])

        load_head(0)
        for h in range(8):
            if h + 1 < 8:
                load_head(h + 1)
            q_ = q_h[h % 2]
            k_ = k_h[h % 2]
            v_ = v_h[h % 2]
            nb_ = nb_h[h % 2]
            og = ostg[h % 2]
            pend = []
            epis = []
            for jq in range(16):
                pend.append(emit_S(q_, k_, jq))
                if len(pend) > 2:
                    epis.append(emit_rest(v_, nb_, og, *pend.pop(0)))
                    if len(epis) > 1:
                        epis.pop(0)()
            while pend:
                epis.append(emit_rest(v_, nb_, og, *pend.pop(0)))
                if len(epis) > 1:
                    epis.pop(0)()
            while epis:
                epis.pop(0)()
            B.dma("sp", on_s[j][:, h * 128:(h + 1) * 128].rearrange("(a p) n -> p a n", p=128), og.ap,
                  DS[15 + h % 2], reads=[og], writes=[S_on[j]])
        B.barrier()
        AR.off = m

    def phase_D(j, js):
        m = AR.off
        bufA = AR.slot([16, 512], BF16)
        bufB = AR.slot([16, 512], BF16)
        hid = AR.slot([16, 512], BF16)
        x1 = [AR.slot([D], F32) for _ in range(4)]
        xn2 = AR.slot([D], BF16)
        tokin = [AR.slot([1024], BF16) for _ in range(4)]
        sgt = [AR.slot([2, 512], BF16) for _ in range(2)]
        t1 = AR.slot([512], F32)
        t2 = AR.slot([512], F32)
        rr = [AR.slot([512], F32) for _ in range(2)]
        tmpa = [AR.slot([512], F32) for _ in range(2)]
        wts = [AR.slot([16, 512], BF16) for _ in range(3)]
        st = AR.slot([4, 8], F32)
        junk = AR.slot([D], BF16)
        wi = [0]
        bi = [0]

        def nbank():
            b = banks[bi[0] % 8]
            bi[0] += 1
            return b

        dspecs = []
        for _tg in range(4):
            for nb in range(4):
                dspecs.append(([(0, 8, wb_pa[nb]), (8, 16, wb_pb[nb])], [W_pa, W_pb]))
            for cg in range(4):
                dspecs.append(([(0, 16, wb_out[cg])], [W_out]))
            for qq in range(4):
                for nb4 in range(4):
                    dspecs.append(([(0, 16, wb_1[qq * 4 + nb4])], [W_1]))
                for cg in range(4):
                    dspecs.append(([(0, 16, wb_2[qq * 4 + cg])], [W_2]))
        dstream = WStream(wts, [DS[4], DS[5], DS[6]], dspecs)

        def wload(src, rd, nk=16):
            return dstream.get()

        tok_sems = [DS[7], DS[8], DS[11], DS[12]]
        tok_loads = [(tg_, si_, tt_) for tg_ in range(4) for si_ in range(2) for tt_ in range(4)]
        tok_state = {"issued": 0}

        def tok_issue():
            k = tok_state["issued"]
            if k >= len(tok_loads):
                return
            tg_, si_, tt_ = tok_loads[k]
            src, sl = (oa_s[j], S_oa[j]) if si_ == 0 else (on_s[j], S_on[j])
            r0 = tg_ * 512 + tt_ * 128
            B.dma("sp", tokin[k % 4].ap, src[r0:r0 + 128, :], tok_sems[k % 4], reads=[sl], writes=[tokin[k % 4]])
            tok_state["issued"] += 1

        for _ in range(4):
            tok_issue()

        def transposes_in(tg, si, src, sl):
            tb = [nbank() for _ in range(4)]
            for tt in range(4):
                k = tg * 8 + si * 4 + tt
                ti = tokin[k % 4]
                fns = []
                for fc in range(8):
                    pv = tb[fc // 2].ap.bitcast(BF16)
                    c0 = (fc % 2) * 512 + tt * 128
                    fns.append(lambda e, pv=pv, fc=fc, c0=c0, ti=ti: e.transpose(
                        pv[:, c0:c0 + 128], ti.ap[:, fc * 128:(fc + 1) * 128], ident.ap))
                B.group("pe", fns, reads=[ti, ident], writes=tb if tt == 0 else [], cont=tb if tt > 0 else [])
                tok_issue()
            for b2 in range(4):
                pv = tb[b2].ap.bitcast(BF16).rearrange("p (a b) -> p a b", a=2, b=512)
                dst = bufA.ap[:, si * 8 + b2 * 2:si * 8 + b2 * 2 + 2, :]
                B.op("dve", lambda e, pv=pv, dst=dst: e.tensor_copy(out=dst, in_=pv), reads=[tb[b2]], writes=[bufA], nowaw=True)

        def rms_stats(tt, c0):
            B.op("act", lambda e: e.activation(out=junk.ap, in_=x1[tt].ap, func=AF.Square, accum_out=st.ap[:, tt, c0:c0 + 1]),
                 reads=[x1[tt], st], writes=[junk, st])
            B.op("act", lambda e: e.activation(out=st.ap[:, tt, c0 + 1:c0 + 2], in_=st.ap[:, tt, c0:c0 + 1], func=AF.Ln,
                                               bias=epsc.ap, scale=1.0 / D), reads=[st, epsc], writes=[st])
            B.op("act", lambda e: e.activation(out=st.ap[:, tt, c0 + 2:c0 + 3], in_=st.ap[:, tt, c0 + 1:c0 + 2], func=AF.Exp, scale=-0.5),
                 reads=[st], writes=[st])

        def accum(bk, tt, cg, gbc, k):
            ta = tmpa[k % 2]
            B.op("dve", lambda e: e.tensor_tensor(out=ta.ap, in0=bk.ap, in1=gbc.ap[:, cg * 512:(cg + 1) * 512], op=ALU.mult),
                 reads=[bk, gbc], writes=[ta])
            B.op("pool", lambda e: e.tensor_tensor(out=x1[tt].ap[:, cg * 512:(cg + 1) * 512],
                                                   in0=x1[tt].ap[:, cg * 512:(cg + 1) * 512], in1=ta.ap, op=ALU.add),
                 reads=[ta, x1[tt]], writes=[x1[tt]])

        for tg in range(4):
            t0 = tg * 512
            B.op("pool", lambda e: e.memset(st.ap, 0.0), reads=[st], writes=[st])
            transposes_in(tg, 0, oa_s[j], S_oa[j])
            transposes_in(tg, 1, on_s[j], S_on[j])
            for nb in range(4):
                wa = wload(None, None)
                wb_ = wa
                for mm in range(4):
                    mc = nb * 4 + mm
                    sg_ = sgt[mc % 2]
                    dsm = DS[9 + mc % 2]
                    B.dma("sp", sg_.ap[:, 0, :], gT[j][mc * 128:(mc + 1) * 128, t0:t0 + 512], dsm, reads=[S_gT[j]], writes=[sg_])
                    B.dma("sp", sg_.ap[:, 1, :], gT[j][2048 + mc * 128:2048 + (mc + 1) * 128, t0:t0 + 512], dsm,
                          reads=[S_gT[j]], writes=[sg_])
                    ba = nbank()
                    bb = nbank()
                    B.group("pe", [lambda e, ba=ba, wa=wa, mm=mm, kc=kc: e.matmul(
                        ba.ap, wa.ap[:, kc, mm * 128:(mm + 1) * 128], bufA.ap[:, kc, :], start=(kc == 0), stop=(kc == 7))
                        for kc in range(8)], reads=[wa, bufA], writes=[ba])
                    B.group("pe", [lambda e, bb=bb, wb_=wb_, mm=mm, kc=kc: e.matmul(
                        bb.ap, wb_.ap[:, 8 + kc, mm * 128:(mm + 1) * 128], bufA.ap[:, 8 + kc, :], start=(kc == 0), stop=(kc == 7))
                        for kc in range(8)], reads=[wb_, bufA], writes=[bb])
                    B.op("dve", lambda e, ba=ba, sg_=sg_: e.tensor_tensor(out=t1.ap, in0=ba.ap, in1=sg_.ap[:, 0, :], op=ALU.mult),
                         reads=[ba, sg_], writes=[t1])
                    B.op("dve", lambda e, bb=bb, sg_=sg_: e.tensor_tensor(out=t2.ap, in0=bb.ap, in1=sg_.ap[:, 1, :], op=ALU.mult),
                         reads=[bb, sg_], writes=[t2])
                    B.op("pool", lambda e, mc=mc: e.tensor_tensor(out=bufB.ap[:, mc, :], in0=t1.ap, in1=t2.ap, op=ALU.add),
                         reads=[t1, t2], writes=[bufB])
            for tt in range(4):
                B.dma("sp", x1[tt].ap, xq[j, t0 + tt * 128:t0 + (tt + 1) * 128, :], DS[16 + tt], writes=[x1[tt]])
            k = 0
            for cg in range(4):
                wo = wload(wb_out[cg], W_out)
                for tt in range(4):
                    bk = nbank()
                    B.group("pe", [lambda e, bk=bk, wo=wo, tt=tt, kc=kc: e.matmul(
                        bk.ap, bufB.ap[:, kc, tt * 128:(tt + 1) * 128], wo.ap[:, kc, :], start=(kc == 0), stop=(kc == 15))
                        for kc in range(16)], reads=[wo, bufB], writes=[bk])
                    accum(bk, tt, cg, gt1_bc, k)
                    k += 1
            for tt in range(4):
                rms_stats(tt, 0)
                B.op("act", lambda e, tt=tt: e.activation(out=xn2.ap, in_=x1[tt].ap, func=AF.Copy, scale=st.ap[:, tt, 2:3]),
                     reads=[x1[tt], st], writes=[xn2])
                for k4 in range(4):
                    bk = nbank()
                    pv = bk.ap.bitcast(BF16)
                    B.group("pe", [lambda e, pv=pv, kk=kk, k4=k4: e.transpose(
                        pv[:, kk * 128:(kk + 1) * 128], xn2.ap[:, (k4 * 4 + kk) * 128:(k4 * 4 + kk + 1) * 128], ident.ap)
                        for kk in range(4)], reads=[xn2, ident], writes=[bk])
                    for kk in range(4):
                        kc = k4 * 4 + kk
                        dst = bufA.ap[:, kc, tt * 128:(tt + 1) * 128]
                        if False:
                            B.op("act", lambda e, pv=pv, kk=kk, kc=kc, dst=dst: e.activation(
                                out=dst, in_=pv[:, kk * 128:(kk + 1) * 128], func=AF.Identity,
                                bias=adaT.ap[:, 48 + kc, js:js + 1], scale=s2p.ap[:, kc, js:js + 1]),
                                reads=[bk, adaT, s2p], writes=[bufA], nowaw=True)
                        else:
                            B.op("dve", lambda e, pv=pv, kk=kk, kc=kc, dst=dst: e.tensor_scalar(
                                out=dst, in0=pv[:, kk * 128:(kk + 1) * 128], scalar1=s2p.ap[:, kc, js:js + 1],
                                scalar2=adaT.ap[:, 48 + kc, js:js + 1], op0=ALU.mult, op1=ALU.add),
                                reads=[bk, adaT, s2p], writes=[bufA], nowaw=True)
            for qq in range(4):
                for nb4 in range(4):
                    w1t = wload(wb_1[qq * 4 + nb4], W_1)
                    for mm in range(4):
                        fc = nb4 * 4 + mm
                        bk = nbank()
                        B.group("pe", [lambda e, bk=bk, w1t=w1t, mm=mm, kc=kc: e.matmul(
                            bk.ap, w1t.ap[:, kc, mm * 128:(mm + 1) * 128], bufA.ap[:, kc, :], start=(kc == 0), stop=(kc == 15))
                            for kc in range(16)], reads=[w1t, bufA], writes=[bk])
                        r_ = rr[fc % 2]
                        B.op("act", lambda e, bk=bk, r_=r_: e.activation(out=r_.ap, in_=bk.ap, func=AF.Relu), reads=[bk], writes=[r_])
                        B.op("pool", lambda e, r_=r_, fc=fc: e.tensor_tensor(out=hid.ap[:, fc, :], in0=r_.ap, in1=r_.ap, op=ALU.mult),
                             reads=[r_], writes=[hid])
                k = 0
                for cg in range(4):
                    w2t = wload(wb_2[qq * 4 + cg], W_2)
                    for tt in range(4):
                        bk = nbank()
                        B.group("pe", [lambda e, bk=bk, w2t=w2t, tt=tt, fc=fc: e.matmul(
                            bk.ap, hid.ap[:, fc, tt * 128:(tt + 1) * 128], w2t.ap[:, fc, :], start=(fc == 0), stop=(fc == 15))
                            for fc in range(16)], reads=[w2t, hid], writes=[bk])
                        accum(bk, tt, cg, gt2_bc, k)
                        k += 1
            for tt in range(4):
                rms_stats(tt, 3)
                B.op("dve", lambda e, tt=tt: e.scalar_tensor_tensor(
                    out=x1[tt].ap, in0=x1[tt].ap, scalar=st.ap[:, tt, 5:6], in1=gf_bc.ap, op0=ALU.mult, op1=ALU.mult),
                    reads=[x1[tt], st, gf_bc], writes=[x1[tt]])
                B.dma("sp", y[j, t0 + tt * 128:t0 + (tt + 1) * 128, :], x1[tt].ap, DS[20 + tt], reads=[x1[tt]], writes=[S_y])
        B.barrier()
        AR.off = m

    if stage >= 1:
        phase_proj(0, xo, 0, full=False)
        AR.n = ARENA_ELEMS
        if os.environ.get("KSTOP") is None:
            phase_proj(0, xq[0], 0, full=True)
    if stage >= 2:
        phase_DA(0)
    if stage >= 3:
        phase_NA(0)
    if stage >= 4:
        make_gt_bc(0)
        phase_D(0, 0)
    if stage >= 5:
        for j in (1, 2):
            phase_proj(j, xq[j], j, full=True)
            phase_DA(j)
            phase_NA(j)
            make_gt_bc(j)
            phase_D(j, j)
    B.barrier()

    with nc.Block() as block:
        B.emit(block)
    es.close()
    return nc, B


def make_in_maps(inputs):
    f = lambda a: np.ascontiguousarray(np.asarray(a, dtype=np.float32))
    x_prompt = f(inputs["x_prompt"])
    x_sample = f(inputs["x_sample"])
    c_prompt = f(inputs["c_prompt"])
    c_sample = f(inputs["c_sample"])
    rpb = f(inputs["rpb"])[0]
    rpb_ext = np.concatenate([rpb.reshape(8, 465), np.full((8, 1), NEG, np.float32)], axis=1)
    common = {
        "w_ada": f(inputs["w_ada"])[0],
        "b_adaT": f(f(inputs["b_ada"])[0].reshape(96, 128).T),
        "g_mixT": f(f(inputs["g_mix"])[0].reshape(16, 128).T),
        "g_mlpT": f(f(inputs["g_mlp"])[0].reshape(16, 128).T),
        "g_final": f(inputs["g_final"]).reshape(1, D),
        "g_subln": f(inputs["g_subln"]).reshape(1, 256),
        "lam4": f(np.stack([f(inputs["lam_q1"])[0], f(inputs["lam_k1"])[0], f(inputs["lam_q2"])[0], f(inputs["lam_k2"])[0]], 0)),
        "w_in": f(inputs["w_in"])[0],
        "w_pa": f(inputs["w_pa"])[0],
        "w_pb": f(inputs["w_pb"])[0],
        "w_out": f(inputs["w_out"])[0],
        "w1": f(inputs["w1"])[0],
        "w2": f(inputs["w2"])[0],
    }
    idx_s = build_na_idx("s", 0)
    idx_s = np.where(idx_s < 0, 465, idx_s)
    nab_s = f(rpb_ext[:, idx_s].transpose(0, 2, 1, 3).reshape(8, 128, 21 * 128))
    nab_p = []
    tabs = []
    for half in range(2):
        idx_p = build_na_idx("p", half)
        idx_p = np.where(idx_p < 0, 465, idx_p)
        nab_p.append(f(rpb_ext[:, idx_p].transpose(0, 2, 1, 3).reshape(8, 128, 27 * 128)))
        tabs.append(build_tables(half))
    maps = []
    for c in range(8):
        b, half = c // 2, c % 2
        xq = np.stack([x_prompt[b, half * T:(half + 1) * T], x_sample[2 * c], x_sample[2 * c + 1]], 0)
        xo = x_prompt[b, (1 - half) * T:(2 - half) * T]
        cs = np.stack([c_prompt[b], c_sample[2 * c], c_sample[2 * c + 1], np.zeros(D, np.float32)], 0)
        cT = f(cs.reshape(4, 16, 128).transpose(2, 1, 0).reshape(128, 64))
        dconst, tabc, tabo, slp = tabs[half]
        mp = dict(common)
        mp.update({"xq": f(xq), "xo": f(xo), "cT": cT, "nab_s": nab_s, "nab_p": nab_p[half],
                   "dconst": f(dconst.reshape(128, 2560)), "tabc": tabc, "tabo": tabo, "slp": slp})
        maps.append(mp)
    return maps


_NC_CACHE = {}


def kernel(**inputs):
    if "nc" not in _NC_CACHE:
        _NC_CACHE["nc"] = build_nc()[0]
    nc = _NC_CACHE["nc"]
    maps = make_in_maps(inputs)
    res = run_bass_kernel_spmd(nc, maps, core_ids=list(range(8)))
    y_prompt = np.zeros((4, 4096, D), np.float32)
    y_sample = np.zeros((16, T, D), np.float32)
    for c in range(8):
        yy = res.results[c]["y"]
        y_prompt[c // 2, (c % 2) * T:(c % 2 + 1) * T] = yy[0]
        y_sample[2 * c] = yy[1]
        y_sample[2 * c + 1] = yy[2]
    return (y_prompt, y_sample)
```
